# Optimizing a Trainium2 kernel written in Bass

```python
import jax, jax.numpy as jnp
from jax import lax
import numpy as np

D_MODEL = 2048
BATCH = 2
SEQ = 16384
DEPTH = 1

N_META = 16
D_MIX = 2 * D_MODEL
D_CONV = D_MIX // 2
CONV_GROUPS = 16
SHORT_CONV_W = 3
D_SSM = D_MIX - D_CONV
SSM_HEAD_DIM = 64
SSM_HEADS = D_SSM // SSM_HEAD_DIM
SSM_GROUPS = 8
SSM_HEADS_PER_GROUP = SSM_HEADS // SSM_GROUPS
SSM_STATE = 128
SSM_CONV_W = 4
CHUNK = 128
SSD_FRONT_PAD = CHUNK - N_META
D_XBC = D_SSM + 2 * SSM_GROUPS * SSM_STATE
D_IN_PROJ = 3 * D_CONV + D_SSM + D_XBC + SSM_HEADS
D_FF = 4 * D_MODEL
EPS = 1e-6
DT_MIN = 1e-3
DT_MAX = 1e-1

kernel_name = "hymba_shortconv_ssd_hybrid_layer"


def rms_norm(x, gain):
    xf = x.astype(jnp.float32)
    y = xf * lax.rsqrt(jnp.mean(xf * xf, axis=-1, keepdims=True) + EPS)
    return (y * gain.astype(jnp.float32)).astype(x.dtype)


def grouped_rms_norm(x, gain, n_groups):
    lead = x.shape[:-1]
    d = x.shape[-1]
    xg = x.astype(jnp.float32).reshape(*lead, n_groups, d // n_groups)
    xg = xg * lax.rsqrt(jnp.mean(xg * xg, axis=-1, keepdims=True) + EPS)
    return (xg.reshape(*lead, d) * gain.astype(jnp.float32)).astype(x.dtype)


def causal_depthwise_conv(x, w):
    k_w = w.shape[0]
    length = x.shape[1]
    xp = jnp.pad(x, ((0, 0), (k_w - 1, 0), (0, 0)))
    y = w[k_w - 1] * x
    for k in range(k_w - 1):
        y = y + w[k] * xp[:, k:k + length]
    return y


def pad_front(a, n):
    return jnp.pad(a, [(0, 0), (n, 0)] + [(0, 0)] * (a.ndim - 2))


def segsum_exp(a):
    q = a.shape[-1]
    cs = jnp.cumsum(a, axis=-1)
    diff = cs[..., :, None] - cs[..., None, :]
    mask = jnp.tril(jnp.ones((q, q), dtype=bool))
    return jnp.exp(jnp.where(mask, diff, -jnp.inf))


def ssd_chunked(x_dt, a_dt, b, c):
    bsz, t_len = x_dt.shape[:2]
    nc = t_len // CHUNK
    g, r, p, n = SSM_GROUPS, SSM_HEADS_PER_GROUP, SSM_HEAD_DIM, SSM_STATE
    x = x_dt.reshape(bsz, nc, CHUNK, g, r, p)
    b = b.reshape(bsz, nc, CHUNK, g, n)
    c = c.reshape(bsz, nc, CHUNK, g, n)
    a = a_dt.reshape(bsz, nc, CHUNK, g, r).transpose(0, 3, 4, 1, 2)
    a_cs = jnp.cumsum(a, axis=-1)

    decay_mat = segsum_exp(a)
    cb = jnp.einsum("bclgn,bcsgn->bgcls", c, b)
    scores = cb[:, :, None] * decay_mat
    y_diag = jnp.einsum("bgrcls,bcsgrp->bclgrp", scores, x)

    decay_to_end = jnp.exp(a_cs[..., -1:] - a_cs).transpose(0, 3, 4, 1, 2)
    states = jnp.einsum("bclgn,bclgrp->bcgrpn", b, x * decay_to_end[..., None])

    chunk_decay = jnp.exp(a_cs[..., -1]).transpose(3, 0, 1, 2)

    def step(h, inp):
        dec, s = inp
        return h * dec[..., None, None] + s, h

    h0 = jnp.zeros((bsz, g, r, p, n), states.dtype)
    _, prev_states = lax.scan(step, h0, (chunk_decay, states.transpose(1, 0, 2, 3, 4, 5)))

    decay_from_start = jnp.exp(a_cs).transpose(0, 3, 4, 1, 2)
    y_off = jnp.einsum("bclgn,cbgrpn->bclgrp", c, prev_states) * decay_from_start[..., None]
    return (y_diag + y_off).reshape(bsz, t_len, SSM_HEADS, p)


def hybrid_mixer(h, w_in, short_conv_w, conv_norm_g, ssm_conv_w, ssm_conv_b,
                 dt_bias, a_log, d_skip, ssm_norm_g, w_out):
    bsz, length, _ = h.shape
    proj = h @ w_in
    o1 = D_CONV
    o2 = 2 * D_CONV
    o3 = 3 * D_CONV
    o4 = o3 + D_SSM
    o5 = o4 + D_XBC
    gate_b, gate_c, v, z, xbc, dt_raw = jnp.split(proj, [o1, o2, o3, o4, o5], axis=-1)

    y_conv = gate_b * causal_depthwise_conv(gate_c * v, short_conv_w)
    y_conv = grouped_rms_norm(y_conv, conv_norm_g, CONV_GROUPS)

    xbc = jax.nn.silu(causal_depthwise_conv(xbc, ssm_conv_w) + ssm_conv_b)
    xs, bs, cs = jnp.split(xbc, [D_SSM, D_SSM + SSM_GROUPS * SSM_STATE], axis=-1)
    xs = xs.reshape(bsz, length, SSM_HEADS, SSM_HEAD_DIM)
    bs = bs.reshape(bsz, length, SSM_GROUPS, SSM_STATE)
    cs = cs.reshape(bsz, length, SSM_GROUPS, SSM_STATE)
    dt = jax.nn.softplus(dt_raw.astype(jnp.float32) + dt_bias.astype(jnp.float32))
    a_neg = -jnp.exp(a_log.astype(jnp.float32))
    x_dt = xs * dt[..., None].astype(xs.dtype)
    a_dt = dt * a_neg
    y_ssm = ssd_chunked(pad_front(x_dt, SSD_FRONT_PAD), pad_front(a_dt, SSD_FRONT_PAD),
                        pad_front(bs, SSD_FRONT_PAD), pad_front(cs, SSD_FRONT_PAD))
    y_ssm = y_ssm[:, SSD_FRONT_PAD:].astype(xs.dtype) + d_skip[:, None] * xs
    y_ssm = y_ssm.reshape(bsz, length, D_SSM)
    y_ssm = grouped_rms_norm(y_ssm * jax.nn.silu(z), ssm_norm_g, SSM_GROUPS)

    return jnp.concatenate([y_conv, y_ssm], axis=-1) @ w_out


def squared_relu_mlp(h, w_ff1, w_ff2):
    return jnp.square(jax.nn.relu(h @ w_ff1)) @ w_ff2


def setup_inputs(seed: int = 0) -> dict:
    key = jax.random.key(seed)
    ks = jax.random.split(key, 18)
    f32 = jnp.float32

    def gain(k, d):
        return 1.0 + 0.02 * jax.random.normal(k, (DEPTH, d), f32)

    dt_init = jnp.exp(jax.random.uniform(ks[7], (DEPTH, SSM_HEADS), f32,
                                         minval=np.log(DT_MIN), maxval=np.log(DT_MAX)))
    dt_bias = dt_init + jnp.log(-jnp.expm1(-dt_init))
    return {
        "x": jax.random.normal(ks[0], (BATCH, SEQ, D_MODEL), f32),
        "meta_tokens": jax.random.normal(ks[1], (N_META, D_MODEL), f32),
        "w_in": jax.random.normal(ks[2], (DEPTH, D_MODEL, D_IN_PROJ), f32) * D_MODEL ** -0.5,
        "short_conv_w": jax.random.normal(ks[3], (DEPTH, SHORT_CONV_W, D_CONV), f32) * SHORT_CONV_W ** -0.5,
        "conv_norm_g": gain(ks[4], D_CONV),
        "ssm_conv_w": jax.random.normal(ks[5], (DEPTH, SSM_CONV_W, D_XBC), f32) * SSM_CONV_W ** -0.5,
        "ssm_conv_b": 0.02 * jax.random.normal(ks[6], (DEPTH, D_XBC), f32),
        "dt_bias": dt_bias,
        "a_log": jnp.log(jax.random.uniform(ks[8], (DEPTH, SSM_HEADS), f32, minval=1.0, maxval=16.0)),
        "d_skip": 1.0 + 0.02 * jax.random.normal(ks[9], (DEPTH, SSM_HEADS), f32),
        "ssm_norm_g": gain(ks[10], D_SSM),
        "w_out": jax.random.normal(ks[11], (DEPTH, D_MIX, D_MODEL), f32) * D_MIX ** -0.5,
        "pre_mix_g": gain(ks[12], D_MODEL),
        "post_mix_g": gain(ks[13], D_MODEL),
        "pre_mlp_g": gain(ks[14], D_MODEL),
        "post_mlp_g": gain(ks[15], D_MODEL),
        "w_ff1": jax.random.normal(ks[16], (DEPTH, D_MODEL, D_FF), f32) * D_MODEL ** -0.5,
        "w_ff2": jax.random.normal(ks[17], (DEPTH, D_FF, D_MODEL), f32) * D_FF ** -0.5,
    }


def reference(x, meta_tokens, w_in, short_conv_w, conv_norm_g, ssm_conv_w, ssm_conv_b,
              dt_bias, a_log, d_skip, ssm_norm_g, w_out, pre_mix_g, post_mix_g,
              pre_mlp_g, post_mlp_g, w_ff1, w_ff2):
    in_dtype = x.dtype
    bsz = x.shape[0]
    meta = jnp.broadcast_to(meta_tokens[None].astype(in_dtype), (bsz, N_META, D_MODEL))
    h = jnp.concatenate([meta, x], axis=1)
    for i in range(DEPTH):
        mix = hybrid_mixer(rms_norm(h, pre_mix_g[i]), w_in[i], short_conv_w[i], conv_norm_g[i],
                           ssm_conv_w[i], ssm_conv_b[i], dt_bias[i], a_log[i], d_skip[i],
                           ssm_norm_g[i], w_out[i])
        h = h + rms_norm(mix, post_mix_g[i])
        ff = squared_relu_mlp(rms_norm(h, pre_mlp_g[i]), w_ff1[i], w_ff2[i])
        h = h + rms_norm(ff, post_mlp_g[i])
    return h[:, N_META:].astype(in_dtype)
```

```python
import os
import numpy as np
from contextlib import ExitStack
import concourse.bass as bass
import concourse.mybir as mybir
from concourse.bass_utils import run_bass_kernel_spmd

F32 = mybir.dt.float32
BF16 = mybir.dt.bfloat16
ALU = mybir.AluOpType
AF = mybir.ActivationFunctionType

ENGS = ["pe", "act", "dve", "pool", "sp"]
INORDER = tuple(os.environ.get("KINORDER", "pe").split(","))

D = 2048
NB, SEQ = 2, 16384
NCORE = 8
TPC = NB * SEQ // NCORE
PRE = 128
PFX = 3 * TPC
NPFXT = PFX // 512
NMASKC = (PFX + PRE) // 128
NT = 512
NTILE = TPC // NT
DCONV = 2048
DSSM = 2048
NH, HP, NG, NS = 32, 64, 8, 128
DXBC = 4096
O_BG, O_CG, O_V, O_Z, O_XBC, O_DT = 0, 2048, 4096, 6144, 8192, 12288
DIN = 12320
DFF = 8192
EPS = 1e-6
HALO = 3
KSTOP = int(os.environ.get("KSTOP", "99"))

PV_PREMIX, PV_POSTMIX, PV_PREMLP, PV_POSTMLP, PV_CNG = 0, 16, 32, 48, 64
PV_SCW = 80
PV_SSMW = 128
PV_SSMB = 256
PV_SNG = 288
PV_N = 304


class Res:
    __slots__ = ("name", "lw", "rd", "excl")

    def __init__(self, name="", excl=False):
        self.name = name
        self.lw = None
        self.rd = {}
        self.excl = excl


class Chan:
    __slots__ = ("key", "cnt")

    def __init__(self, key):
        self.key = key
        self.cnt = 0


class Sched:
    def __init__(self, nc):
        self.nc = nc
        self.q = {e: [] for e in ENGS}
        self.cnt = {e: 0 for e in ENGS}
        self.known = {e: {} for e in ENGS}
        self.chans = []
        self.nins = 0

    def chan(self):
        c = Chan("d%d" % len(self.chans))
        self.chans.append(c)
        return c

    def _waits(self, eng, deps):
        best = {}
        for d in deps:
            if d is None:
                continue
            k, v = d
            if k == eng and eng in INORDER:
                continue
            if best.get(k, 0) < v:
                best[k] = v
        waits = []
        kn = self.known[eng]
        for k, v in best.items():
            if kn.get(k, 0) >= v:
                continue
            kn[k] = v
            waits.append((k, v))
        return waits

    def emit(self, eng, fn, reads=(), writes=(), signal=True):
        if any(r.excl for r in reads):
            writes = list(writes) + [r for r in reads if r.excl]
            reads = [r for r in reads if not r.excl]
        deps = []
        for r in reads:
            deps.append(r.lw)
        for w in writes:
            deps.append(w.lw)
            deps.extend(w.rd.values())
        waits = self._waits(eng, deps)
        if signal:
            self.cnt[eng] += 1
            tok = (eng, self.cnt[eng])
        else:
            tok = (eng, self.cnt[eng] + 1)
        for r in reads:
            r.rd[eng] = tok
        for w in writes:
            w.lw = tok
            w.rd = {}
        self.q[eng].append((waits, fn, eng if signal else None, 1))
        self.nins += 1
        return tok

    def dma(self, eng, chan, out, in_, reads=(), writes=()):
        deps = [(chan.key, chan.cnt)] if chan.cnt else []
        for r in reads:
            deps.append(r.lw)
        for w in writes:
            deps.append(w.lw)
            deps.extend(w.rd.values())
        waits = self._waits(eng, deps)
        chan.cnt += 16
        tok = (chan.key, chan.cnt)
        for r in reads:
            r.rd[chan.key] = tok
        for w in writes:
            w.lw = tok
            w.rd = {}
        self.q[eng].append((waits, lambda e: e.dma_start(out=out, in_=in_), chan.key, 16))
        self.nins += 1
        return tok

    def raw(self, eng, chan, fn, reads=(), writes=()):
        deps = [(chan.key, chan.cnt)] if chan.cnt else []
        for r in reads:
            deps.append(r.lw)
        for w in writes:
            deps.append(w.lw)
            deps.extend(w.rd.values())
        waits = self._waits(eng, deps)
        chan.cnt += 16
        tok = (chan.key, chan.cnt)
        for r in reads:
            r.rd[chan.key] = tok
        for w in writes:
            w.lw = tok
            w.rd = {}
        self.q[eng].append((waits, fn, chan.key, 16))
        return tok

    def wait(self, eng, deps):
        waits = self._waits(eng, deps)
        if waits:
            self.q[eng].append((waits, None, None, 0))

    def run(self, stack):
        nc = self.nc
        sems = {}
        for k in ENGS + [c.key for c in self.chans]:
            sems[k] = stack.enter_context(nc.semaphore("s_" + k))
        block = stack.enter_context(nc.Block())

        def body(eng):
            def f(e):
                for waits, fn, sigkey, inc in self.q[eng]:
                    for k, v in waits:
                        e.wait_ge(sems[k], v)
                    if fn is None:
                        continue
                    ins = fn(e)
                    if sigkey is not None:
                        ins.then_inc(sems[sigkey], inc)
            return f

        block.tensor(body("pe"))
        block.scalar(body("act"))
        block.vector(body("dve"))
        block.gpsimd(body("pool"))
        block.sync(body("sp"))


class T:
    __slots__ = ("t", "r")

    def __init__(self, t, name=""):
        self.t = t
        self.r = Res(name)


def build_program(exchange=True, ntile=NTILE):
    nc = bass.Bass("TRN2", target_bir_lowering=False)
    TOT = PFX + PRE + TPC
    xT = nc.dram_tensor("xT", [D, TOT], F32, kind="ExternalInput").ap()
    w_in = nc.dram_tensor("w_in", [D, DIN], F32, kind="ExternalInput").ap()
    w_out = nc.dram_tensor("w_out", [2 * D, D], F32, kind="ExternalInput").ap()
    w_ff1 = nc.dram_tensor("w_ff1", [D, DFF], F32, kind="ExternalInput").ap()
    w_ff2 = nc.dram_tensor("w_ff2", [DFF, D], F32, kind="ExternalInput").ap()
    pvec_d = nc.dram_tensor("pvec", [128, PV_N], F32, kind="ExternalInput").ap()
    hvec_d = nc.dram_tensor("hvec", [96], F32, kind="ExternalInput").ap()
    cst_d = nc.dram_tensor("cst", [128, 512], F32, kind="ExternalInput").ap()
    pmask_d = nc.dram_tensor("pmask", [128, NMASKC], F32, kind="ExternalInput").ap()
    xmask_d = nc.dram_tensor("xmask", [72], F32, kind="ExternalInput").ap()
    outT = nc.dram_tensor("outT", [D, TPC], F32, kind="ExternalOutput").ap()
    XW = DSSM + NH
    if exchange:
        ex_src = nc.dram_tensor("ex_src", [128, XW], F32).ap()
        ex_dst = nc.dram_tensor("ex_dst", [NCORE * 128, XW], F32).ap()

    with ExitStack() as st:
        S = Sched(nc)

        def sb(name, shape, dt):
            return T(st.enter_context(nc.sbuf_tensor("sb_" + name, shape, dt)), name)

        def ps(name, shape, dt):
            t = T(st.enter_context(nc.psum_tensor("ps_" + name, shape, dt)), name)
            t.r.excl = True
            return t

        def mm(out, lhsT, rhs, start, stop, reads, writes, signal=True):
            S.emit("pe", lambda e: e.matmul(out, lhsT=lhsT, rhs=rhs, start=start, stop=stop),
                   reads, writes, signal)

        def tp(out, in_, reads, writes):
            S.emit("pe", lambda e: e.transpose(out, in_, ident_ap), list(reads) + [cst.r], writes)

        def act(out, in_, func, reads, writes, bias=None, scale=None, accum=None):
            kw = {}
            if bias is not None:
                kw["bias"] = bias
            if scale is not None:
                kw["scale"] = scale
            if accum is not None:
                kw["accum_out"] = accum
            S.emit("act", lambda e: e.activation(out=out, in_=in_, func=func, **kw), reads, writes)

        def tt(out, in0, in1, op, reads, writes):
            S.emit("dve", lambda e: e.tensor_tensor(out=out, in0=in0, in1=in1, op=op), reads, writes)

        def ts(out, in0, s1, op0, reads, writes, s2=None, op1=None):
            if op1 is None:
                S.emit("dve", lambda e: e.tensor_scalar(out=out, in0=in0, scalar1=s1, scalar2=None, op0=op0),
                       reads, writes)
            else:
                S.emit("dve", lambda e: e.tensor_scalar(out=out, in0=in0, scalar1=s1, scalar2=s2, op0=op0, op1=op1),
                       reads, writes)

        def stt(out, in0, scalar, in1, op0, op1, reads, writes):
            S.emit("dve", lambda e: e.scalar_tensor_tensor(out=out, in0=in0, scalar=scalar, in1=in1, op0=op0, op1=op1),
                   reads, writes)

        def cpy(out, in_, reads, writes):
            S.emit("dve", lambda e: e.tensor_copy(out=out, in_=in_), reads, writes)

        xb = [sb("xb%d" % j, [128, NT], F32) for j in range(16)]
        ob = [sb("ob%d" % j, [128, NT], F32) for j in range(16)]
        hb = [sb("hb%d" % j, [128, NT], BF16) for j in range(16)]
        mix = [sb("mix%d" % j, [128, NT], BF16) for j in range(32)]
        NSLOT = 7
        wsl = [sb("ws%d" % i, [128, 8, 256], BF16) for i in range(NSLOT)]
        wch = [S.chan() for _ in range(NSLOT)]
        wstate = {"i": 0}
        pg = [ps("pg%d" % i, [128, 512], F32) for i in range(4)]
        pgstate = {"i": 0}
        pS0 = ps("pS0", [128, 512], F32)
        pS1 = ps("pS1", [128, 512], F32)
        pS2 = ps("pS2", [128, 512], F32)
        pS3 = ps("pS3", [128, 512], F32)
        r_xsT = r_BT = r_CB = pS0.r
        r_D = [pS1.r for _ in range(4)]
        r_yd = r_yo = pS2.r
        r_st = r_yT = pS3.r

        cst = sb("cst", [128, 512], F32)
        tri_ap = cst.t[:, 0:128]
        strict_ap = cst.t[:, 128:256]
        ones_ap = cst.t[:, 256:384]
        ident_ap = cst.t[:, 384:512]

        onesb = sb("onesb", [128, 128], BF16)
        pvec = sb("pvec", [128, PV_N], F32)
        hvec = sb("hvec", [128, 96], F32)
        aneg = sb("aneg", [128, 32], F32)
        pmask = sb("pmask", [128, NMASKC], F32)
        xmask = sb("xmask", [128, 72], F32)
        sq = [sb("sq%d" % i, [128, NT], BF16) for i in range(2)]
        rstd = sb("rstd", [128, NT], F32)
        csb = [sb("csb%d" % i, [128, NT], F32) for i in range(2)]
        cvb = [sb("cvb%d" % i, [128, HALO + NT], F32) for i in range(2)]
        cacc = [sb("cacc%d" % i, [128, NT], F32) for i in range(2)]
        xcb = [sb("xcb%d" % i, [128, HALO + NT], F32) for i in range(4)]
        Bb = sb("Bb", [128, NT], BF16)
        Cb = sb("Cb", [128, NT], BF16)
        class _View:
            def __init__(self, base):
                self.t = base.t[:, HALO:HALO + NT]
                self.r = base.r
        xsF = [_View(xcb[0]), _View(xcb[1])]
        BF = _View(xcb[2])
        halo_cv = sb("halo_cv", [128, 16, HALO], F32)
        halo_x = sb("halo_x", [128, 32, HALO], F32)
        NCH = NT // 128
        zs = [sb("zs%d" % c, [128, 256], F32) for c in range(NCH)]
        dtc = [sb("dtc%d" % c, [128, 32], F32) for c in range(NCH)]
        ac = [sb("ac%d" % c, [128, 32], F32) for c in range(NCH)]
        acs = [sb("acs%d" % c, [128, 32], F32) for c in range(NCH)]
        eacs = [sb("eacs%d" % c, [128, 32], F32) for c in range(NCH)]
        dtdte = [sb("dtdte%d" % c, [128, 32], F32) for c in range(NCH)]
        decb = [sb("decb%d" % c, [128, 32], F32) for c in range(NCH)]
        sm1 = sb("sm1", [128, 32], F32)
        sm2 = sb("sm2", [128, 32], F32)
        ltot = sb("ltot", [128, 32], F32)
        x_tok = sb("x_tok", [128, 256], F32)
        xdt = sb("xdt", [128, 256], BF16)
        xdec = sb("xdec", [128, 256], BF16)
        B_tok = sb("B_tok", [128, 128], BF16)
        CBm = sb("CBm", [128, 128], F32)
        lhD = [sb("lhD%d" % i, [128, 128], F32) for i in range(2)]
        Eh = [sb("Eh%d" % i, [128, 128], F32) for i in range(2)]
        scT = [sb("scT%d" % i, [128, 128], BF16) for i in range(2)]
        ysb = sb("ysb", [128, 256], F32)
        yt2 = sb("yt2", [128, 256], F32)
        ssq = sb("ssq", [128, 1], F32)
        grs = sb("grs", [128, 1], F32)
        hstate = sb("hstate", [128, DSSM], F32)
        hprev = sb("hprev", [128, DSSM], BF16)
        r_hs = [Res() for _ in range(NG)]
        r_hp = [Res() for _ in range(NG)]

        c_const = S.chan()
        c_x = [S.chan() for _ in range(16)]
        c_o = [S.chan() for _ in range(16)]

        S.dma("sp", c_const, cst.t[:], cst_d, writes=[cst.r])
        S.dma("sp", c_const, pvec.t[:], pvec_d, writes=[pvec.r])
        S.dma("sp", c_const, hvec.t[:], hvec_d.partition_broadcast(128), writes=[hvec.r])
        S.dma("sp", c_const, pmask.t[:], pmask_d, writes=[pmask.r])
        S.dma("sp", c_const, xmask.t[:], xmask_d.partition_broadcast(128), writes=[xmask.r])
        cpy(onesb.t[:], ones_ap, [cst.r], [onesb.r])
        act(aneg.t[:], hvec.t[:, 32:64], AF.Exp, [hvec.r], [aneg.r])
        ts(aneg.t[:], aneg.t[:], -1.0, ALU.mult, [aneg.r], [aneg.r])
        S.emit("dve", lambda e: e.memset(halo_cv.t[:], 0.0), [], [halo_cv.r])
        S.emit("dve", lambda e: e.memset(halo_x.t[:], 0.0), [], [halo_x.r])
        S.emit("dve", lambda e: e.memset(hstate.t[:], 0.0), [], r_hs)
        S.emit("dve", lambda e: e.memset(hprev.t[:], 0.0), [], r_hp)
        S.emit("dve", lambda e: e.memset(ltot.t[:], 0.0), [], [ltot.r])

        def pv(col):
            return pvec.t[:, col:col + 1]

        def wload(src, ncols):
            i = wstate["i"] % NSLOT
            wstate["i"] += 1
            S.dma("pool", wch[i], wsl[i].t[:, :, 0:ncols], src.rearrange("(kc p) n -> p kc n", p=128),
                  writes=[wsl[i].r])
            return wsl[i]

        def next_pg():
            b = pg[pgstate["i"] % 4]
            pgstate["i"] += 1
            return b

        def proj_fm(W, row0, nk, col0, ncols, src, n, banks=None, first=True, last=True):
            nchunk = ncols // 128
            if banks is None:
                banks = [next_pg() for _ in range(nchunk)]
            slots = [wload(W[row0 + h * 1024: row0 + (h + 1) * 1024, col0:col0 + ncols], ncols) for h in range(nk)]
            for ci in range(nchunk):
                for kc in range(nk * 8):
                    sl = slots[kc // 8]
                    mm(banks[ci].t[:, 0:n], sl.t[:, kc % 8, ci * 128:(ci + 1) * 128], src[kc].t[:, 0:n],
                       start=(first and kc == 0), stop=(last and kc == nk * 8 - 1),
                       reads=[sl.r, src[kc].r], writes=[banks[ci].r], signal=(kc % 8 == 7))
            return banks

        def sumsq_rstd(srcs, n, dim, pre=None):
            bank = next_pg()
            for j, (ap, r) in enumerate(srcs):
                s = sq[j % 2]
                act(s.t[:, 0:n], ap, AF.Square, [r], [s.r])
                mm(bank.t[:, 0:n], onesb.t[:], s.t[:, 0:n], start=(j == 0), stop=(j == len(srcs) - 1),
                   reads=[onesb.r, s.r], writes=[bank.r], signal=True)
            act(rstd.t[:, 0:n], bank.t[:, 0:n], AF.Ln, [bank.r], [rstd.r], bias=EPS, scale=1.0 / dim)
            act(rstd.t[:, 0:n], rstd.t[:, 0:n], AF.Exp, [rstd.r], [rstd.r], scale=-0.5)

        def conv_fm(buf, hal, hidx, n, wcols, bias_col, out_ap, out_res, func, save_halo=True):
            K = len(wcols)
            S.emit("act", lambda e: e.activation(out=buf.t[:, 0:HALO], in_=hal.t[:, hidx, :], func=AF.Copy),
                   [hal.r], [buf.r])
            acc = cacc[hidx % 2]
            if bias_col is None:
                ts(acc.t[:, 0:n], buf.t[:, HALO:HALO + n], pv(wcols[K - 1]), ALU.mult, [buf.r, pvec.r], [acc.r])
            else:
                ts(acc.t[:, 0:n], buf.t[:, HALO:HALO + n], pv(wcols[K - 1]), ALU.mult, [buf.r, pvec.r], [acc.r],
                   s2=pv(bias_col), op1=ALU.add)
            for d in range(1, K):
                stt(acc.t[:, 0:n], buf.t[:, HALO - d:HALO - d + n], pv(wcols[K - 1 - d]), acc.t[:, 0:n],
                    ALU.mult, ALU.add, [buf.r, pvec.r, acc.r], [acc.r])
            if save_halo:
                S.emit("act", lambda e: e.activation(out=hal.t[:, hidx, :], in_=buf.t[:, n:n + HALO], func=AF.Copy),
                       [buf.r], [hal.r])
            if func is not None:
                act(out_ap, acc.t[:, 0:n], func, [acc.r], [out_res])
            return acc

        def tile(t0, n, mode, premask):
            nch = n // 128
            full = mode == "full"
            for j in range(16):
                S.dma("sp", c_x[j], xb[j].t[:, 0:n], xT[j * 128:(j + 1) * 128, t0:t0 + n], writes=[xb[j].r])
            sumsq_rstd([(xb[j].t[:, 0:n], xb[j].r) for j in range(16)], n, D)
            for j in range(16):
                stt(hb[j].t[:, 0:n], xb[j].t[:, 0:n], pv(PV_PREMIX + j), rstd.t[:, 0:n], ALU.mult, ALU.mult,
                    [xb[j].r, pvec.r, rstd.r], [hb[j].r])

            if KSTOP <= 1:
                return
            if mode != "state":
                for jj in range(8):
                    PC = proj_fm(w_in, 0, 2, O_CG + jj * 256, 256, hb, n)
                    for ci in range(2):
                        act(csb[ci].t[:, 0:n], PC[ci].t[:, 0:n], AF.Copy, [PC[ci].r], [csb[ci].r])
                    PVb = proj_fm(w_in, 0, 2, O_V + jj * 256, 256, hb, n)
                    accs = []
                    for ci in range(2):
                        j = jj * 2 + ci
                        tt(cvb[ci].t[:, HALO:HALO + n], csb[ci].t[:, 0:n], PVb[ci].t[:, 0:n], ALU.mult,
                           [csb[ci].r, PVb[ci].r], [cvb[ci].r])
                        if full:
                            accs.append(conv_fm(cvb[ci], halo_cv, j, n,
                                                [PV_SCW + k * 16 + j for k in range(3)], None, None, None, None))
                        else:
                            S.emit("act", (lambda ci, j: lambda e: e.activation(
                                out=halo_cv.t[:, j, :], in_=cvb[ci].t[:, n:n + HALO], func=AF.Copy))(ci, j),
                                [cvb[ci].r], [halo_cv.r])
                    if not full:
                        continue
                    PB = proj_fm(w_in, 0, 2, O_BG + jj * 256, 256, hb, n)
                    for ci in range(2):
                        j = jj * 2 + ci
                        acc = accs[ci]
                        tt(acc.t[:, 0:n], acc.t[:, 0:n], PB[ci].t[:, 0:n], ALU.mult, [acc.r, PB[ci].r], [acc.r])
                        sumsq_rstd([(acc.t[:, 0:n], acc.r)], n, 128)
                        stt(mix[j].t[:, 0:n], acc.t[:, 0:n], pv(PV_CNG + j), rstd.t[:, 0:n], ALU.mult, ALU.mult,
                            [acc.r, pvec.r, rstd.r], [mix[j].r])

            if KSTOP <= 2:
                return
            dslots = [wload(w_in[h * 1024:(h + 1) * 1024, O_DT:O_DT + 32], 32) for h in range(2)]
            for c in range(nch):
                bank = next_pg()
                for kc in range(16):
                    sl = dslots[kc // 8]
                    mm(bank.t[:, 0:32], hb[kc].t[:, c * 128:(c + 1) * 128], sl.t[:, kc % 8, 0:32],
                       start=(kc == 0), stop=(kc == 15), reads=[sl.r, hb[kc].r], writes=[bank.r],
                       signal=(kc % 8 == 7))
                tt(sm1.t[:], bank.t[:, 0:32], hvec.t[:, 0:32], ALU.add, [bank.r, hvec.r], [sm1.r])
                act(sm1.t[:], sm1.t[:], AF.Exp, [sm1.r], [sm1.r])
                act(dtc[c].t[:], sm1.t[:], AF.Ln, [sm1.r], [dtc[c].r], bias=1.0)
                if premask:
                    gc = t0 // 128 + c
                    ts(dtc[c].t[:], dtc[c].t[:], pmask.t[:, gc:gc + 1], ALU.mult, [dtc[c].r, pmask.r], [dtc[c].r])
                tt(ac[c].t[:], dtc[c].t[:], aneg.t[:], ALU.mult, [dtc[c].r, aneg.r], [ac[c].r])
                b2 = next_pg()
                mm(b2.t[:, 0:32], tri_ap, ac[c].t[:], True, True, [cst.r, ac[c].r], [b2.r])
                mm(b2.t[:, 32:64], ones_ap, ac[c].t[:], True, True, [cst.r, ac[c].r], [b2.r])
                cpy(acs[c].t[:], b2.t[:, 0:32], [b2.r], [acs[c].r])
                act(eacs[c].t[:], b2.t[:, 0:32], AF.Exp, [b2.r], [eacs[c].r])
                act(decb[c].t[:], b2.t[:, 32:64], AF.Exp, [b2.r], [decb[c].r])
                tt(ltot.t[:], ltot.t[:], b2.t[:, 32:64], ALU.add, [ltot.r, b2.r], [ltot.r])
                tt(sm2.t[:], b2.t[:, 32:64], acs[c].t[:], ALU.subtract, [b2.r, acs[c].r], [sm2.r])
                act(sm2.t[:], sm2.t[:], AF.Exp, [sm2.r], [sm2.r])
                tt(dtdte[c].t[:], sm2.t[:], dtc[c].t[:], ALU.mult, [sm2.r, dtc[c].r], [dtdte[c].r])

            if KSTOP <= 3:
                return
            for g in range(NG):
                PX = proj_fm(w_in, 0, 2, O_XBC + g * 256, 256, hb, n)
                for i in range(2):
                    chn = 2 * g + i
                    act(xcb[i].t[:, HALO:HALO + n], PX[i].t[:, 0:n], AF.Copy, [PX[i].r], [xcb[i].r])
                    conv_fm(xcb[i], halo_x, chn, n, [PV_SSMW + k * 32 + chn for k in range(4)], PV_SSMB + chn,
                            xsF[i].t[:, 0:n], xsF[i].r, AF.Silu)
                PBm = proj_fm(w_in, 0, 2, O_XBC + 2048 + g * 128, 128, hb, n)
                chn = 16 + g
                act(xcb[2].t[:, HALO:HALO + n], PBm[0].t[:, 0:n], AF.Copy, [PBm[0].r], [xcb[2].r])
                conv_fm(xcb[2], halo_x, chn, n, [PV_SSMW + k * 32 + chn for k in range(4)], PV_SSMB + chn,
                        BF.t[:, 0:n], BF.r, AF.Silu)
                if mode != "state":
                    PCm = proj_fm(w_in, 0, 2, O_XBC + 3072 + g * 128, 128, hb, n)
                    chn = 24 + g
                    act(xcb[3].t[:, HALO:HALO + n], PCm[0].t[:, 0:n], AF.Copy, [PCm[0].r], [xcb[3].r])
                    conv_fm(xcb[3], halo_x, chn, n, [PV_SSMW + k * 32 + chn for k in range(4)], PV_SSMB + chn,
                            Cb.t[:, 0:n], Cb.r, AF.Silu)
                if full:
                    cpy(Bb.t[:, 0:n], BF.t[:, 0:n], [BF.r], [Bb.r])
                    zsl = [wload(w_in[h * 1024:(h + 1) * 1024, O_Z + g * 256:O_Z + (g + 1) * 256], 256)
                           for h in range(2)]
                    for c in range(nch):
                        bank = next_pg()
                        for kc in range(16):
                            sl = zsl[kc // 8]
                            mm(bank.t[:, 0:256], hb[kc].t[:, c * 128:(c + 1) * 128], sl.t[:, kc % 8, 0:256],
                               start=(kc == 0), stop=(kc == 15), reads=[sl.r, hb[kc].r], writes=[bank.r],
                               signal=(kc % 8 == 7))
                        act(zs[c].t[:], bank.t[:, 0:256], AF.Silu, [bank.r], [zs[c].r])

                gs = slice(g * 256, (g + 1) * 256)
                h4 = slice(g * 4, (g + 1) * 4)

                def bc4(tl):
                    return tl.t[:, h4].unsqueeze(2).to_broadcast([128, 4, 64])

                def v3(ap):
                    return ap.rearrange("p (h j) -> p h j", h=4)

                for c in range(nch):
                    cs = slice(c * 128, (c + 1) * 128)
                    for i in range(2):
                        tp(pS0.t[:, i * 128:(i + 1) * 128], xsF[i].t[:, cs], [xsF[i].r], [r_xsT])
                    tp(pS0.t[:, 256:384], BF.t[:, cs], [BF.r], [r_BT])
                    act(x_tok.t[:], pS0.t[:, 0:256], AF.Copy, [r_xsT], [x_tok.r])
                    act(B_tok.t[:], pS0.t[:, 256:384], AF.Copy, [r_BT], [B_tok.r])
                    tt(v3(xdec.t[:]), v3(x_tok.t[:]), bc4(dtdte[c]), ALU.mult, [x_tok.r, dtdte[c].r], [xdec.r])
                    mm(pS3.t[:, 0:256], B_tok.t[:], xdec.t[:], True, True, [B_tok.r, xdec.r], [r_st])
                    if full:
                        tt(v3(xdt.t[:]), v3(x_tok.t[:]), bc4(dtc[c]), ALU.mult, [x_tok.r, dtc[c].r], [xdt.r])
                        mm(pS0.t[:, 384:512], Bb.t[:, cs], Cb.t[:, cs], True, True, [Bb.r, Cb.r], [r_CB])
                        tt(CBm.t[:], pS0.t[:, 384:512], tri_ap, ALU.mult, [r_CB, cst.r], [CBm.r])
                        mm(pS2.t[:, 256:512], Cb.t[:, cs], hprev.t[:, gs], True, True, [Cb.r, r_hp[g]], [r_yo])
                        for r in range(4):
                            h = g * 4 + r
                            k2 = r % 2
                            ts(lhD[k2].t[:], strict_ap, ac[c].t[:, h:h + 1], ALU.mult, [cst.r, ac[c].r], [lhD[k2].r])
                            mm(pS1.t[:, r * 128:(r + 1) * 128], lhD[k2].t[:], tri_ap, True, True,
                               [lhD[k2].r, cst.r], [r_D[r]])
                            act(Eh[k2].t[:], pS1.t[:, r * 128:(r + 1) * 128], AF.Exp, [r_D[r]], [Eh[k2].r])
                            tt(scT[k2].t[:], Eh[k2].t[:], CBm.t[:], ALU.mult, [Eh[k2].r, CBm.r], [scT[k2].r])
                            mm(pS2.t[:, r * 64:(r + 1) * 64], scT[k2].t[:], xdt.t[:, r * 64:(r + 1) * 64], True, True,
                               [scT[k2].r, xdt.r], [r_yd])
                        tt(v3(ysb.t[:]), v3(pS2.t[:, 256:512]), bc4(eacs[c]), ALU.mult, [r_yo, eacs[c].r], [ysb.r])
                        tt(ysb.t[:], ysb.t[:], pS2.t[:, 0:256], ALU.add, [ysb.r, r_yd], [ysb.r])
                        tt(v3(yt2.t[:]), v3(x_tok.t[:]),
                           hvec.t[:, 64 + g * 4:64 + (g + 1) * 4].unsqueeze(2).to_broadcast([128, 4, 64]),
                           ALU.mult, [x_tok.r, hvec.r], [yt2.r])
                        tt(ysb.t[:], ysb.t[:], yt2.t[:], ALU.add, [ysb.r, yt2.r], [ysb.r])
                        tt(ysb.t[:], ysb.t[:], zs[c].t[:], ALU.mult, [ysb.r, zs[c].r], [ysb.r])
                        act(yt2.t[:], ysb.t[:], AF.Square, [ysb.r], [yt2.r, ssq.r], accum=ssq.t[:])
                        act(grs.t[:], ssq.t[:], AF.Ln, [ssq.r], [grs.r], bias=EPS, scale=1.0 / 256)
                        act(grs.t[:], grs.t[:], AF.Exp, [grs.r], [grs.r], scale=-0.5)
                        ts(yt2.t[:], ysb.t[:], grs.t[:, 0:1], ALU.mult, [ysb.r, grs.r], [yt2.r])
                        for i in range(2):
                            tp(pS3.t[:, 256 + i * 128:256 + (i + 1) * 128], yt2.t[:, i * 128:(i + 1) * 128],
                               [yt2.r], [r_yT])
                        for i in range(2):
                            act(mix[16 + 2 * g + i].t[:, cs], pS3.t[:, 256 + i * 128:256 + (i + 1) * 128], AF.Copy,
                                [r_yT, pvec.r], [mix[16 + 2 * g + i].r], scale=pv(PV_SNG + 2 * g + i))
                    tt(v3(hstate.t[:, gs]), v3(hstate.t[:, gs]), bc4(decb[c]), ALU.mult, [r_hs[g], decb[c].r], [r_hs[g]])
                    tt(hstate.t[:, gs], hstate.t[:, gs], pS3.t[:, 0:256], ALU.add, [r_hs[g], r_st], [r_hs[g]])
                    if mode != "state":
                        act(hprev.t[:, gs], hstate.t[:, gs], AF.Copy, [r_hs[g]], [r_hp[g]])
            if not full or KSTOP <= 4:
                return

            for cb in range(8):
                banks = [next_pg() for _ in range(2)]
                for hf in range(4):
                    proj_fm(w_out, hf * 1024, 1, cb * 256, 256, mix[hf * 8:(hf + 1) * 8], n, banks=banks,
                            first=(hf == 0), last=(hf == 3))
                for ci in range(2):
                    act(ob[cb * 2 + ci].t[:, 0:n], banks[ci].t[:, 0:n], AF.Copy, [banks[ci].r], [ob[cb * 2 + ci].r])
            sumsq_rstd([(ob[j].t[:, 0:n], ob[j].r) for j in range(16)], n, D)
            for j in range(16):
                stt(ob[j].t[:, 0:n], ob[j].t[:, 0:n], pv(PV_POSTMIX + j), rstd.t[:, 0:n], ALU.mult, ALU.mult,
                    [ob[j].r, pvec.r, rstd.r], [ob[j].r])
                tt(xb[j].t[:, 0:n], xb[j].t[:, 0:n], ob[j].t[:, 0:n], ALU.add, [xb[j].r, ob[j].r], [xb[j].r])
            sumsq_rstd([(xb[j].t[:, 0:n], xb[j].r) for j in range(16)], n, D)
            for j in range(16):
                stt(hb[j].t[:, 0:n], xb[j].t[:, 0:n], pv(PV_PREMLP + j), rstd.t[:, 0:n], ALU.mult, ALU.mult,
                    [xb[j].r, pvec.r, rstd.r], [hb[j].r])
            for half in range(2):
                for cb in range(16):
                    banks = proj_fm(w_ff1, 0, 2, half * 4096 + cb * 256, 256, hb, n)
                    for ci in range(2):
                        m = mix[cb * 2 + ci]
                        act(csb[ci].t[:, 0:n], banks[ci].t[:, 0:n], AF.Relu, [banks[ci].r], [csb[ci].r])
                        tt(m.t[:, 0:n], csb[ci].t[:, 0:n], csb[ci].t[:, 0:n], ALU.mult, [csb[ci].r], [m.r])
                for cb in range(8):
                    banks = [next_pg() for _ in range(2)]
                    for hf in range(4):
                        proj_fm(w_ff2, half * 4096 + hf * 1024, 1, cb * 256, 256, mix[hf * 8:(hf + 1) * 8], n,
                                banks=banks, first=(hf == 0), last=(hf == 3))
                    for ci in range(2):
                        o = ob[cb * 2 + ci]
                        if half == 0:
                            act(o.t[:, 0:n], banks[ci].t[:, 0:n], AF.Copy, [banks[ci].r], [o.r])
                        else:
                            tt(o.t[:, 0:n], o.t[:, 0:n], banks[ci].t[:, 0:n], ALU.add, [o.r, banks[ci].r], [o.r])
            sumsq_rstd([(ob[j].t[:, 0:n], ob[j].r) for j in range(16)], n, D)
            for j in range(16):
                stt(ob[j].t[:, 0:n], ob[j].t[:, 0:n], pv(PV_POSTMLP + j), rstd.t[:, 0:n], ALU.mult, ALU.mult,
                    [ob[j].r, pvec.r, rstd.r], [ob[j].r])
                tt(ob[j].t[:, 0:n], ob[j].t[:, 0:n], xb[j].t[:, 0:n], ALU.add, [ob[j].r, xb[j].r], [ob[j].r])
                S.dma("sp", c_o[j], outT[j * 128:(j + 1) * 128, t0 - PRE - PFX:t0 - PRE - PFX + n], ob[j].t[:, 0:n],
                      reads=[ob[j].r])

        if exchange and ntile >= 0:
            tile(0, PRE, "state", True)
            for ti in range(ntile):
                tile(PRE + ti * NT, NT, "state", False)
            c_ex = S.chan()
            S.dma("sp", c_ex, ex_src[:, 0:DSSM], hstate.t[:], reads=r_hs)
            S.dma("sp", c_ex, ex_src[:, DSSM:XW], ltot.t[:], reads=[ltot.r])
            r_exd = Res()
            c_cc = S.chan()
            S.wait("pool", [(c_ex.key, c_ex.cnt)])
            S.raw("pool", c_cc, lambda e: e.collective_compute(
                "AllGather", ALU.bypass, replica_groups=[list(range(NCORE))],
                ins=[ex_src], outs=[ex_dst]), writes=[r_exd])
            lall = sb("lall", [128, NCORE, NH], F32)
            for j in range(NCORE):
                S.dma("sp", c_ex, lall.t[:, j, :], ex_dst[j * 128:(j + 1) * 128, DSSM:XW], reads=[r_exd],
                      writes=[lall.r])
            coef = sb("coef", [128, NH], F32)
            sj = ob
            S.emit("dve", lambda e: e.memset(hstate.t[:], 0.0), [], r_hs)
            for j in range(NCORE):
                ts(coef.t[:], lall.t[:, 0, :], xmask.t[:, 8 + j * 8:8 + j * 8 + 1], ALU.mult,
                   [lall.r, xmask.r], [coef.r])
                for k in range(1, NCORE):
                    stt(coef.t[:], lall.t[:, k, :], xmask.t[:, 8 + j * 8 + k:8 + j * 8 + k + 1], coef.t[:],
                        ALU.mult, ALU.add, [lall.r, xmask.r, coef.r], [coef.r])
                act(coef.t[:], coef.t[:], AF.Exp, [coef.r], [coef.r])
                ts(coef.t[:], coef.t[:], xmask.t[:, j:j + 1], ALU.mult, [coef.r, xmask.r], [coef.r])
                for qd in range(4):
                    S.dma("sp", c_ex, sj[qd].t[:, 0:512], ex_dst[j * 128:(j + 1) * 128, qd * 512:(qd + 1) * 512],
                          reads=[r_exd], writes=[sj[qd].r])
                    tt(sj[qd].t[:, 0:512].rearrange("p (h j) -> p h j", h=8),
                       sj[qd].t[:, 0:512].rearrange("p (h j) -> p h j", h=8),
                       coef.t[:, qd * 8:(qd + 1) * 8].unsqueeze(2).to_broadcast([128, 8, 64]), ALU.mult,
                       [sj[qd].r, coef.r], [sj[qd].r])
                    tt(hstate.t[:, qd * 512:(qd + 1) * 512], hstate.t[:, qd * 512:(qd + 1) * 512], sj[qd].t[:, 0:512],
                       ALU.add, [r_hs[2 * qd], r_hs[2 * qd + 1], sj[qd].r], [r_hs[2 * qd], r_hs[2 * qd + 1]])
            for g in range(NG):
                act(hprev.t[:, g * 256:(g + 1) * 256], hstate.t[:, g * 256:(g + 1) * 256], AF.Copy, [r_hs[g]], [r_hp[g]])
            S.emit("dve", lambda e: e.memset(halo_cv.t[:], 0.0), [], [halo_cv.r])
            S.emit("dve", lambda e: e.memset(halo_x.t[:], 0.0), [], [halo_x.r])

        if not exchange and ntile >= 0:
            npf = NPFXT if ntile == NTILE else 0
            for ti in range(NPFXT - npf, NPFXT):
                tile(ti * NT, NT, "state", True)
        if ntile >= 0:
            tile(PFX, PRE, "pre", True)
        for ti in range(ntile):
            tile(PFX + PRE + ti * NT, NT, "full", False)
        S.wait("sp", [(c.key, c.cnt) for c in c_o])
        print("instructions:", S.nins, {e: len(S.q[e]) for e in ENGS}, flush=True)
        S.run(st)
    return nc


_CACHE = {}


def _host_consts():
    k = np.arange(128)
    tri = (k[:, None] <= k[None, :]).astype(np.float32)
    strict = (k[:, None] > k[None, :]).astype(np.float32)
    ones = np.ones((128, 128), np.float32)
    ident = np.eye(128, dtype=np.float32)
    return np.ascontiguousarray(np.concatenate([tri, strict, ones, ident], axis=1))


def _fm(v):
    return np.ascontiguousarray(np.asarray(v, np.float32).reshape(-1, 128).T)


def kernel(x, meta_tokens, w_in, short_conv_w, conv_norm_g, ssm_conv_w, ssm_conv_b,
           dt_bias, a_log, d_skip, ssm_norm_g, w_out, pre_mix_g, post_mix_g,
           pre_mlp_g, post_mlp_g, w_ff1, w_ff2, _exchange=False, _ntile=NTILE):
    x = np.asarray(x, np.float32)
    meta = np.asarray(meta_tokens, np.float32)
    key = (_exchange, _ntile)
    if key not in _CACHE:
        _CACHE[key] = build_program(_exchange, _ntile)
    nc = _CACHE[key]
    pvec = np.concatenate(
        [_fm(pre_mix_g[0]), _fm(post_mix_g[0]), _fm(pre_mlp_g[0]), _fm(post_mlp_g[0]), _fm(conv_norm_g[0])]
        + [_fm(short_conv_w[0, k]) for k in range(3)]
        + [_fm(ssm_conv_w[0, k]) for k in range(4)]
        + [_fm(ssm_conv_b[0]), _fm(ssm_norm_g[0])], axis=1)
    assert pvec.shape == (128, PV_N), pvec.shape
    pvec = np.ascontiguousarray(pvec, dtype=np.float32)
    hvec = np.concatenate([np.asarray(dt_bias[0]), np.asarray(a_log[0]), np.asarray(d_skip[0])]).astype(np.float32)
    cst = _host_consts()
    W_in = np.ascontiguousarray(np.asarray(w_in[0], np.float32))
    W_out = np.ascontiguousarray(np.asarray(w_out[0], np.float32))
    W1 = np.ascontiguousarray(np.asarray(w_ff1[0], np.float32))
    W2 = np.ascontiguousarray(np.asarray(w_ff2[0], np.float32))
    in_maps = []
    for c in range(NCORE):
        b, q = divmod(c, NCORE // NB)
        L = PFX + PRE + TPC
        nreal = (q + 1) * TPC
        xt = np.zeros((D, L), np.float32)
        xt[:, L - nreal:] = x[b, :nreal].T
        xt[:, L - nreal - 16:L - nreal] = meta.T
        tm = np.zeros(PFX + PRE, np.float32)
        tm[L - nreal - 16:] = 1.0
        pm = np.ascontiguousarray(tm.reshape(NMASKC, 128).T)
        sel = np.zeros(8, np.float32)
        M = np.zeros((8, 8), np.float32)
        for j in range(NCORE):
            if j // 4 == b and j < c:
                sel[j] = 1.0
                for k2 in range(j + 1, c):
                    M[j, k2] = 1.0
        xm = np.concatenate([sel, M.reshape(-1)]).astype(np.float32)
        in_maps.append({"xT": xt, "w_in": W_in, "w_out": W_out, "w_ff1": W1, "w_ff2": W2, "pvec": pvec,
                        "hvec": hvec, "cst": cst, "pmask": pm, "xmask": xm})
    res = run_bass_kernel_spmd(nc, in_maps, core_ids=list(range(NCORE)))
    out = np.empty((NB, SEQ, D), np.float32)
    nt = _ntile * NT
    for c in range(NCORE):
        b, q = divmod(c, NCORE // NB)
        out[b, q * TPC:q * TPC + nt] = res.results[c]["outT"][:, :nt].T
    return out
```

```python
import os
import numpy as np
from contextlib import ExitStack
import concourse.bass as bass
import concourse.mybir as mybir
from concourse.bass_utils import run_bass_kernel_spmd

F32 = mybir.dt.float32
BF16 = mybir.dt.bfloat16
ALU = mybir.AluOpType
AF = mybir.ActivationFunctionType

ENGS = ["pe", "act", "dve", "pool", "sp"]
INORDER = tuple(os.environ.get("KINORDER", "pe").split(","))

D = 2048
NB, SEQ = 2, 16384
NCORE = 8
TPC = NB * SEQ // NCORE
PRE = 128
PFX = 3 * TPC
NPFXT = PFX // 512
NMASKC = (PFX + PRE) // 128
NT = 512
NTILE = TPC // NT
DCONV = 2048
DSSM = 2048
NH, HP, NG, NS = 32, 64, 8, 128
DXBC = 4096
O_BG, O_CG, O_V, O_Z, O_XBC, O_DT = 0, 2048, 4096, 6144, 8192, 12288
DIN = 12320
DFF = 8192
EPS = 1e-6
HALO = 3
KSTOP = int(os.environ.get("KSTOP", "99"))
KWG = int(os.environ.get("KWG", "16"))

PV_PREMIX, PV_POSTMIX, PV_PREMLP, PV_POSTMLP, PV_CNG = 0, 16, 32, 48, 64
PV_SCW = 80
PV_SSMW = 128
PV_SSMB = 256
PV_SNG = 288
PV_N = 304


class Res:
    __slots__ = ("name", "lw", "rd", "excl")

    def __init__(self, name="", excl=False):
        self.name = name
        self.lw = None
        self.rd = {}
        self.excl = excl


class Chan:
    __slots__ = ("key", "cnt")

    def __init__(self, key):
        self.key = key
        self.cnt = 0


class Sched:
    def __init__(self, nc):
        self.nc = nc
        self.q = {e: [] for e in ENGS}
        self.cnt = {e: 0 for e in ENGS}
        self.known = {e: {} for e in ENGS}
        self.chans = []
        self.nins = 0

    def chan(self):
        c = Chan("d%d" % len(self.chans))
        self.chans.append(c)
        return c

    def _waits(self, eng, deps):
        best = {}
        for d in deps:
            if d is None:
                continue
            k, v = d
            if k == eng and eng in INORDER:
                continue
            if best.get(k, 0) < v:
                best[k] = v
        waits = []
        kn = self.known[eng]
        for k, v in best.items():
            if kn.get(k, 0) >= v:
                continue
            kn[k] = v
            waits.append((k, v))
        return waits

    def emit(self, eng, fn, reads=(), writes=(), signal=True):
        if any(r.excl for r in reads):
            writes = list(writes) + [r for r in reads if r.excl]
            reads = [r for r in reads if not r.excl]
        deps = []
        for r in reads:
            deps.append(r.lw)
        for w in writes:
            deps.append(w.lw)
            deps.extend(w.rd.values())
        waits = self._waits(eng, deps)
        if signal:
            self.cnt[eng] += 1
            tok = (eng, self.cnt[eng])
        else:
            tok = (eng, self.cnt[eng] + 1)
        for r in reads:
            r.rd[eng] = tok
        for w in writes:
            w.lw = tok
            w.rd = {}
        self.q[eng].append((waits, fn, eng if signal else None, 1))
        self.nins += 1
        return tok

    def dma(self, eng, chan, out, in_, reads=(), writes=()):
        deps = [(chan.key, chan.cnt)] if chan.cnt else []
        for r in reads:
            deps.append(r.lw)
        for w in writes:
            deps.append(w.lw)
            deps.extend(w.rd.values())
        waits = self._waits(eng, deps)
        chan.cnt += 16
        tok = (chan.key, chan.cnt)
        for r in reads:
            r.rd[chan.key] = tok
        for w in writes:
            w.lw = tok
            w.rd = {}
        self.q[eng].append((waits, lambda e: e.dma_start(out=out, in_=in_), chan.key, 16))
        self.nins += 1
        return tok

    def raw(self, eng, chan, fn, reads=(), writes=()):
        deps = [(chan.key, chan.cnt)] if chan.cnt else []
        for r in reads:
            deps.append(r.lw)
        for w in writes:
            deps.append(w.lw)
            deps.extend(w.rd.values())
        waits = self._waits(eng, deps)
        chan.cnt += 16
        tok = (chan.key, chan.cnt)
        for r in reads:
            r.rd[chan.key] = tok
        for w in writes:
            w.lw = tok
            w.rd = {}
        self.q[eng].append((waits, fn, chan.key, 16))
        return tok

    def wait(self, eng, deps):
        waits = self._waits(eng, deps)
        if waits:
            self.q[eng].append((waits, None, None, 0))

    def run(self, stack):
        nc = self.nc
        sems = {}
        for k in ENGS + [c.key for c in self.chans]:
            sems[k] = stack.enter_context(nc.semaphore("s_" + k))
        block = stack.enter_context(nc.Block())

        def body(eng):
            def f(e):
                for waits, fn, sigkey, inc in self.q[eng]:
                    for k, v in waits:
                        e.wait_ge(sems[k], v)
                    if fn is None:
                        continue
                    ins = fn(e)
                    if sigkey is not None:
                        ins.then_inc(sems[sigkey], inc)
            return f

        block.tensor(body("pe"))
        block.scalar(body("act"))
        block.vector(body("dve"))
        block.gpsimd(body("pool"))
        block.sync(body("sp"))


class T:
    __slots__ = ("t", "r")

    def __init__(self, t, name=""):
        self.t = t
        self.r = Res(name)


def build_program(exchange=True, ntile=NTILE):
    nc = bass.Bass("TRN2", target_bir_lowering=False)
    TOT = PFX + PRE + TPC
    xT = nc.dram_tensor("xT", [D, TOT], F32, kind="ExternalInput").ap()
    w_in = nc.dram_tensor("w_in", [D, DIN], F32, kind="ExternalInput").ap()
    w_out = nc.dram_tensor("w_out", [2 * D, D], F32, kind="ExternalInput").ap()
    w_ff1 = nc.dram_tensor("w_ff1", [D, DFF], F32, kind="ExternalInput").ap()
    w_ff2 = nc.dram_tensor("w_ff2", [DFF, D], F32, kind="ExternalInput").ap()
    pvec_d = nc.dram_tensor("pvec", [128, PV_N], F32, kind="ExternalInput").ap()
    hvec_d = nc.dram_tensor("hvec", [96], F32, kind="ExternalInput").ap()
    cst_d = nc.dram_tensor("cst", [128, 512], F32, kind="ExternalInput").ap()
    pmask_d = nc.dram_tensor("pmask", [128, NMASKC], F32, kind="ExternalInput").ap()
    xmask_d = nc.dram_tensor("xmask", [72], F32, kind="ExternalInput").ap()
    outT = nc.dram_tensor("outT", [D, TPC], F32, kind="ExternalOutput").ap()
    XW = DSSM + NH
    if exchange:
        ex_src = nc.dram_tensor("ex_src", [128, XW], F32).ap()
        ex_dst = nc.dram_tensor("ex_dst", [NCORE * 128, XW], F32).ap()

    with ExitStack() as st:
        S = Sched(nc)

        def sb(name, shape, dt):
            return T(st.enter_context(nc.sbuf_tensor("sb_" + name, shape, dt)), name)

        def ps(name, shape, dt):
            t = T(st.enter_context(nc.psum_tensor("ps_" + name, shape, dt)), name)
            t.r.excl = True
            return t

        def mm(out, lhsT, rhs, start, stop, reads, writes, signal=True):
            S.emit("pe", lambda e: e.matmul(out, lhsT=lhsT, rhs=rhs, start=start, stop=stop),
                   reads, writes, signal)

        def tp(out, in_, reads, writes):
            S.emit("pe", lambda e: e.transpose(out, in_, ident_ap), list(reads) + [cst.r], writes)

        def act(out, in_, func, reads, writes, bias=None, scale=None, accum=None):
            kw = {}
            if bias is not None:
                kw["bias"] = bias
            if scale is not None:
                kw["scale"] = scale
            if accum is not None:
                kw["accum_out"] = accum
            S.emit("act", lambda e: e.activation(out=out, in_=in_, func=func, **kw), reads, writes)

        def tt(out, in0, in1, op, reads, writes):
            S.emit("dve", lambda e: e.tensor_tensor(out=out, in0=in0, in1=in1, op=op), reads, writes)

        def ts(out, in0, s1, op0, reads, writes, s2=None, op1=None):
            if op1 is None:
                S.emit("dve", lambda e: e.tensor_scalar(out=out, in0=in0, scalar1=s1, scalar2=None, op0=op0),
                       reads, writes)
            else:
                S.emit("dve", lambda e: e.tensor_scalar(out=out, in0=in0, scalar1=s1, scalar2=s2, op0=op0, op1=op1),
                       reads, writes)

        def stt(out, in0, scalar, in1, op0, op1, reads, writes):
            S.emit("dve", lambda e: e.scalar_tensor_tensor(out=out, in0=in0, scalar=scalar, in1=in1, op0=op0, op1=op1),
                   reads, writes)

        def cpy(out, in_, reads, writes):
            S.emit("dve", lambda e: e.tensor_copy(out=out, in_=in_), reads, writes)

        xb = [sb("xb%d" % j, [128, NT], F32) for j in range(16)]
        ob = [sb("ob%d" % j, [128, NT], F32) for j in range(16)]
        hb = [sb("hb%d" % j, [128, NT], BF16) for j in range(16)]
        mix = [sb("mix%d" % j, [128, NT], BF16) for j in range(32)]
        NSLOT = 6
        wsl = [sb("ws%d" % i, [128, 8, 256], BF16) for i in range(NSLOT)]
        wch = [S.chan() for _ in range(NSLOT)]
        wstate = {"i": 0}
        pg = [ps("pg%d" % i, [128, 512], F32) for i in range(4)]
        pgstate = {"i": 0}
        pS0 = ps("pS0", [128, 512], F32)
        pS1 = ps("pS1", [128, 512], F32)
        pS2 = ps("pS2", [128, 512], F32)
        pS3 = ps("pS3", [128, 512], F32)
        r_xsT = r_BT = r_CB = pS0.r
        r_D = [pS1.r for _ in range(4)]
        r_yd = r_yo = pS2.r
        r_st = r_yT = pS3.r

        cst = sb("cst", [128, 512], F32)
        tri_ap = cst.t[:, 0:128]
        strict_ap = cst.t[:, 128:256]
        ones_ap = cst.t[:, 256:384]
        ident_ap = cst.t[:, 384:512]

        onesb = sb("onesb", [128, 128], BF16)
        pvec = sb("pvec", [128, PV_N], F32)
        hvec = sb("hvec", [128, 96], F32)
        aneg = sb("aneg", [128, 32], F32)
        pmask = sb("pmask", [128, NMASKC], F32)
        xmask = sb("xmask", [128, 72], F32)
        sq = [sb("sq%d" % i, [128, NT], BF16) for i in range(2)]
        rstd = sb("rstd", [128, NT], F32)
        cvb = [sb("cvb%d" % i, [128, HALO + NT], F32) for i in range(2)]
        cacc = [sb("cacc%d" % i, [128, NT], F32) for i in range(2)]
        xcbS = [[sb("xcb%d_%d" % (b_, i), [128, HALO + NT], F32) for i in range(4)] for b_ in range(2)]
        BbS = [sb("Bb%d" % b_, [128, NT], BF16) for b_ in range(2)]
        CbS = [sb("Cb%d" % b_, [128, NT], BF16) for b_ in range(2)]
        halo_cv = sb("halo_cv", [128, 16, HALO], F32)
        halo_x = sb("halo_x", [128, 32, HALO], F32)
        NCH = NT // 128
        zsS = [[sb("zs%d_%d" % (b_, c), [128, 256], BF16) for c in range(NCH)] for b_ in range(2)]
        dtc = [sb("dtc%d" % c, [128, 32], F32) for c in range(NCH)]
        ac = [sb("ac%d" % c, [128, 32], F32) for c in range(NCH)]
        acs = [sb("acs%d" % c, [128, 32], F32) for c in range(NCH)]
        eacs = [sb("eacs%d" % c, [128, 32], F32) for c in range(NCH)]
        dtdte = [sb("dtdte%d" % c, [128, 32], F32) for c in range(NCH)]
        decb = [sb("decb%d" % c, [128, 32], F32) for c in range(NCH)]
        sm1 = sb("sm1", [128, 32], F32)
        sm2 = sb("sm2", [128, 32], F32)
        ltot = sb("ltot", [128, 32], F32)
        x_tok = sb("x_tok", [128, 256], F32)
        xdt = sb("xdt", [128, 256], BF16)
        xdec = sb("xdec", [128, 256], BF16)
        B_tok = sb("B_tok", [128, 128], BF16)
        CBm = sb("CBm", [128, 128], F32)
        lhD = [sb("lhD%d" % i, [128, 128], F32) for i in range(4)]
        Eh = [sb("Eh%d" % i, [128, 128], F32) for i in range(2)]
        scT = [sb("scT%d" % i, [128, 128], BF16) for i in range(2)]
        ysb = sb("ysb", [128, 256], F32)
        yt2 = sb("yt2", [128, 256], F32)
        ssq = sb("ssq", [128, 1], F32)
        grs = sb("grs", [128, 1], F32)
        hstate = sb("hstate", [128, DSSM], F32)
        hprev = sb("hprev", [128, DSSM], BF16)
        r_hs = [Res() for _ in range(NG)]
        r_hp = [Res() for _ in range(NG)]

        c_const = S.chan()
        c_x = [S.chan() for _ in range(16)]
        c_o = [S.chan() for _ in range(16)]

        S.dma("sp", c_const, cst.t[:], cst_d, writes=[cst.r])
        S.dma("sp", c_const, pvec.t[:], pvec_d, writes=[pvec.r])
        S.dma("sp", c_const, hvec.t[:], hvec_d.partition_broadcast(128), writes=[hvec.r])
        S.dma("sp", c_const, pmask.t[:], pmask_d, writes=[pmask.r])
        S.dma("sp", c_const, xmask.t[:], xmask_d.partition_broadcast(128), writes=[xmask.r])
        cpy(onesb.t[:], ones_ap, [cst.r], [onesb.r])
        act(aneg.t[:], hvec.t[:, 32:64], AF.Exp, [hvec.r], [aneg.r])
        ts(aneg.t[:], aneg.t[:], -1.0, ALU.mult, [aneg.r], [aneg.r])
        S.emit("dve", lambda e: e.memset(halo_cv.t[:], 0.0), [], [halo_cv.r])
        S.emit("dve", lambda e: e.memset(halo_x.t[:], 0.0), [], [halo_x.r])
        S.emit("dve", lambda e: e.memset(hstate.t[:], 0.0), [], r_hs)
        S.emit("dve", lambda e: e.memset(hprev.t[:], 0.0), [], r_hp)
        S.emit("dve", lambda e: e.memset(ltot.t[:], 0.0), [], [ltot.r])

        def pv(col):
            return pvec.t[:, col:col + 1]

        def wload(src, ncols):
            i = wstate["i"] % NSLOT
            wstate["i"] += 1
            S.dma("pool", wch[i], wsl[i].t[:, :, 0:ncols], src.rearrange("(kc p) n -> p kc n", p=128),
                  writes=[wsl[i].r])
            return wsl[i]

        def next_pg():
            b = pg[pgstate["i"] % 4]
            pgstate["i"] += 1
            return b

        def proj_fm(W, row0, nk, col0, ncols, src, n, banks=None, first=True, last=True):
            nchunk = ncols // 128
            if banks is None:
                banks = [next_pg() for _ in range(nchunk)]
            slots = [wload(W[row0 + h * 1024: row0 + (h + 1) * 1024, col0:col0 + ncols], ncols) for h in range(nk)]
            for ci in range(nchunk):
                for kc in range(nk * 8):
                    sl = slots[kc // 8]
                    mm(banks[ci].t[:, 0:n], sl.t[:, kc % 8, ci * 128:(ci + 1) * 128], src[kc].t[:, 0:n],
                       start=(first and kc == 0), stop=(last and kc == nk * 8 - 1),
                       reads=[sl.r, src[kc].r], writes=[banks[ci].r], signal=(kc % 8 == 7))
                    if kc % KWG == KWG - 1:
                        yield (last and kc == nk * 8 - 1)
            return banks

        def run(gen):
            try:
                while True:
                    next(gen)
            except StopIteration as e:
                return e.value

        def chain(*gens):
            for g_ in gens:
                yield from g_

        def weave(main, side):
            if os.environ.get("KNOWEAVE"):
                run(main)
                run(side)
                return
            ma = sa = True
            mb, sn = True, False
            while ma or sa:
                if ma and not (sa and sn and mb):
                    try:
                        v = next(main)
                        if v is not None:
                            mb = bool(v)
                    except StopIteration:
                        ma, mb = False, True
                if sa and (mb or not sn):
                    try:
                        sn = bool(next(side))
                    except StopIteration:
                        sa, sn = False, False

        def sumsq_rstd(srcs, n, dim, pre=None):
            bank = next_pg()
            for j, (ap, r) in enumerate(srcs):
                s = sq[j % 2]
                act(s.t[:, 0:n], ap, AF.Square, [r], [s.r])
                mm(bank.t[:, 0:n], onesb.t[:], s.t[:, 0:n], start=(j == 0), stop=(j == len(srcs) - 1),
                   reads=[onesb.r, s.r], writes=[bank.r], signal=True)
            act(rstd.t[:, 0:n], bank.t[:, 0:n], AF.Ln, [bank.r], [rstd.r], bias=EPS, scale=1.0 / dim)
            act(rstd.t[:, 0:n], rstd.t[:, 0:n], AF.Exp, [rstd.r], [rstd.r], scale=-0.5)

        def conv_fm(buf, hal, hidx, n, wcols, bias_col, out_ap, out_res, func, save_halo=True):
            K = len(wcols)
            S.emit("act", lambda e: e.activation(out=buf.t[:, 0:HALO], in_=hal.t[:, hidx, :], func=AF.Copy),
                   [hal.r], [buf.r])
            acc = cacc[hidx % 2]
            if bias_col is None:
                ts(acc.t[:, 0:n], buf.t[:, HALO:HALO + n], pv(wcols[K - 1]), ALU.mult, [buf.r, pvec.r], [acc.r])
            else:
                ts(acc.t[:, 0:n], buf.t[:, HALO:HALO + n], pv(wcols[K - 1]), ALU.mult, [buf.r, pvec.r], [acc.r],
                   s2=pv(bias_col), op1=ALU.add)
            for d in range(1, K):
                stt(acc.t[:, 0:n], buf.t[:, HALO - d:HALO - d + n], pv(wcols[K - 1 - d]), acc.t[:, 0:n],
                    ALU.mult, ALU.add, [buf.r, pvec.r, acc.r], [acc.r])
            if save_halo:
                S.emit("act", lambda e: e.activation(out=hal.t[:, hidx, :], in_=buf.t[:, n:n + HALO], func=AF.Copy),
                       [buf.r], [hal.r])
            if func is not None:
                act(out_ap, acc.t[:, 0:n], func, [acc.r], [out_res])
            return acc

        def tile(t0, n, mode, premask):
            nch = n // 128
            full = mode == "full"
            for j in range(16):
                S.dma("sp", c_x[j], xb[j].t[:, 0:n], xT[j * 128:(j + 1) * 128, t0:t0 + n], writes=[xb[j].r])
            sumsq_rstd([(xb[j].t[:, 0:n], xb[j].r) for j in range(16)], n, D)
            for j in range(16):
                stt(hb[j].t[:, 0:n], xb[j].t[:, 0:n], pv(PV_PREMIX + j), rstd.t[:, 0:n], ALU.mult, ALU.mult,
                    [xb[j].r, pvec.r, rstd.r], [hb[j].r])

            dslots = [wload(w_in[h * 1024:(h + 1) * 1024, O_DT:O_DT + 32], 32) for h in range(2)]
            for c in range(nch):
                bank = next_pg()
                for kc in range(16):
                    sl = dslots[kc // 8]
                    mm(bank.t[:, 0:32], hb[kc].t[:, c * 128:(c + 1) * 128], sl.t[:, kc % 8, 0:32],
                       start=(kc == 0), stop=(kc == 15), reads=[sl.r, hb[kc].r], writes=[bank.r],
                       signal=(kc % 8 == 7))
                tt(sm1.t[:], bank.t[:, 0:32], hvec.t[:, 0:32], ALU.add, [bank.r, hvec.r], [sm1.r])
                act(sm1.t[:], sm1.t[:], AF.Exp, [sm1.r], [sm1.r])
                act(dtc[c].t[:], sm1.t[:], AF.Ln, [sm1.r], [dtc[c].r], bias=1.0)
                if premask:
                    gc = t0 // 128 + c
                    ts(dtc[c].t[:], dtc[c].t[:], pmask.t[:, gc:gc + 1], ALU.mult, [dtc[c].r, pmask.r], [dtc[c].r])
                tt(ac[c].t[:], dtc[c].t[:], aneg.t[:], ALU.mult, [dtc[c].r, aneg.r], [ac[c].r])
                b2 = next_pg()
                mm(b2.t[:, 0:32], tri_ap, ac[c].t[:], True, True, [cst.r, ac[c].r], [b2.r])
                mm(b2.t[:, 32:64], ones_ap, ac[c].t[:], True, True, [cst.r, ac[c].r], [b2.r])
                cpy(acs[c].t[:], b2.t[:, 0:32], [b2.r], [acs[c].r])
                act(eacs[c].t[:], b2.t[:, 0:32], AF.Exp, [b2.r], [eacs[c].r])
                act(decb[c].t[:], b2.t[:, 32:64], AF.Exp, [b2.r], [decb[c].r])
                tt(sm2.t[:], b2.t[:, 32:64], acs[c].t[:], ALU.subtract, [b2.r, acs[c].r], [sm2.r])
                act(sm2.t[:], sm2.t[:], AF.Exp, [sm2.r], [sm2.r])
                tt(dtdte[c].t[:], sm2.t[:], dtc[c].t[:], ALU.mult, [sm2.r, dtc[c].r], [dtdte[c].r])

            def branchA(jj):
                PC = yield from proj_fm(w_in, 0, 2, O_CG + jj * 256, 256, hb, n)
                for ci in range(2):
                    act(cacc[ci].t[:, 0:n], PC[ci].t[:, 0:n], AF.Copy, [PC[ci].r], [cacc[ci].r])
                    yield
                PVb = yield from proj_fm(w_in, 0, 2, O_V + jj * 256, 256, hb, n)
                for ci in range(2):
                    j = jj * 2 + ci
                    tt(cvb[ci].t[:, HALO:HALO + n], cacc[ci].t[:, 0:n], PVb[ci].t[:, 0:n], ALU.mult,
                       [cacc[ci].r, PVb[ci].r], [cvb[ci].r])
                    yield
                    if full:
                        conv_fm(cvb[ci], halo_cv, j, n, [PV_SCW + k * 16 + j for k in range(3)], None, None, None, None)
                    else:
                        S.emit("act", (lambda ci, j: lambda e: e.activation(
                            out=halo_cv.t[:, j, :], in_=cvb[ci].t[:, n:n + HALO], func=AF.Copy))(ci, j),
                            [cvb[ci].r], [halo_cv.r])
                    yield
                if not full:
                    return
                PB = yield from proj_fm(w_in, 0, 2, O_BG + jj * 256, 256, hb, n)
                for ci in range(2):
                    j = jj * 2 + ci
                    acc = cacc[ci]
                    tt(acc.t[:, 0:n], acc.t[:, 0:n], PB[ci].t[:, 0:n], ALU.mult, [acc.r, PB[ci].r], [acc.r])
                    yield
                    sumsq_rstd([(acc.t[:, 0:n], acc.r)], n, 128)
                    yield
                    stt(mix[j].t[:, 0:n], acc.t[:, 0:n], pv(PV_CNG + j), rstd.t[:, 0:n], ALU.mult, ALU.mult,
                        [acc.r, pvec.r, rstd.r], [mix[j].r])
                    yield

            def prep(g, bs):
                X = xcbS[bs]
                PX = yield from proj_fm(w_in, 0, 2, O_XBC + g * 256, 256, hb, n)
                for i in range(2):
                    chn = 2 * g + i
                    act(X[i].t[:, HALO:HALO + n], PX[i].t[:, 0:n], AF.Copy, [PX[i].r], [X[i].r])
                    yield
                    conv_fm(X[i], halo_x, chn, n, [PV_SSMW + k * 32 + chn for k in range(4)], PV_SSMB + chn,
                            X[i].t[:, HALO:HALO + n], X[i].r, AF.Silu)
                    yield
                PBm = yield from proj_fm(w_in, 0, 2, O_XBC + 2048 + g * 128, 128, hb, n)
                chn = 16 + g
                act(X[2].t[:, HALO:HALO + n], PBm[0].t[:, 0:n], AF.Copy, [PBm[0].r], [X[2].r])
                yield
                conv_fm(X[2], halo_x, chn, n, [PV_SSMW + k * 32 + chn for k in range(4)], PV_SSMB + chn,
                        X[2].t[:, HALO:HALO + n], X[2].r, AF.Silu)
                yield
                if mode != "state":
                    PCm = yield from proj_fm(w_in, 0, 2, O_XBC + 3072 + g * 128, 128, hb, n)
                    chn = 24 + g
                    act(X[3].t[:, HALO:HALO + n], PCm[0].t[:, 0:n], AF.Copy, [PCm[0].r], [X[3].r])
                    yield
                    conv_fm(X[3], halo_x, chn, n, [PV_SSMW + k * 32 + chn for k in range(4)], PV_SSMB + chn,
                            CbS[bs].t[:, 0:n], CbS[bs].r, AF.Silu)
                    yield
                if full:
                    cpy(BbS[bs].t[:, 0:n], X[2].t[:, HALO:HALO + n], [X[2].r], [BbS[bs].r])
                    yield
                    zsl = [wload(w_in[h * 1024:(h + 1) * 1024, O_Z + g * 256:O_Z + (g + 1) * 256], 256)
                           for h in range(2)]
                    for c in range(nch):
                        bank = next_pg()
                        for kc in range(16):
                            sl = zsl[kc // 8]
                            mm(bank.t[:, 0:256], hb[kc].t[:, c * 128:(c + 1) * 128], sl.t[:, kc % 8, 0:256],
                               start=(kc == 0), stop=(kc == 15), reads=[sl.r, hb[kc].r], writes=[bank.r],
                               signal=(kc % 8 == 7))
                            if kc % KWG == KWG - 1:
                                yield (kc == 15)
                        act(zsS[bs][c].t[:], bank.t[:, 0:256], AF.Silu, [bank.r], [zsS[bs][c].r])
                        yield

            def ssd(g, bs):
                X = xcbS[bs]
                Bb, Cb, zs = BbS[bs], CbS[bs], zsS[bs]
                gs = slice(g * 256, (g + 1) * 256)
                h4 = slice(g * 4, (g + 1) * 4)

                def bc4(tl):
                    return tl.t[:, h4].unsqueeze(2).to_broadcast([128, 4, 64])

                def v3(ap):
                    return ap.rearrange("p (h j) -> p h j", h=4)

                for c in range(nch):
                    cs = slice(c * 128, (c + 1) * 128)
                    xc = slice(HALO + c * 128, HALO + (c + 1) * 128)
                    yield True
                    for i in range(2):
                        tp(pS0.t[:, i * 128:(i + 1) * 128], X[i].t[:, xc], [X[i].r], [r_xsT])
                    tp(pS0.t[:, 256:384], X[2].t[:, xc], [X[2].r], [r_BT])
                    yield
                    act(B_tok.t[:], pS0.t[:, 256:384], AF.Copy, [r_BT], [B_tok.r])
                    yield
                    if full:
                        act(x_tok.t[:], pS0.t[:, 0:256], AF.Copy, [r_xsT], [x_tok.r])
                        yield
                        tt(v3(xdec.t[:]), v3(x_tok.t[:]), bc4(dtdte[c]), ALU.mult, [x_tok.r, dtdte[c].r], [xdec.r])
                    else:
                        tt(v3(xdec.t[:]), v3(pS0.t[:, 0:256]), bc4(dtdte[c]), ALU.mult, [r_xsT, dtdte[c].r], [xdec.r])
                    yield
                    mm(pS3.t[:, 0:256], B_tok.t[:], xdec.t[:], True, True, [B_tok.r, xdec.r], [r_st])
                    yield
                    if full:
                        tt(v3(xdt.t[:]), v3(x_tok.t[:]), bc4(dtc[c]), ALU.mult, [x_tok.r, dtc[c].r], [xdt.r])
                        yield
                        mm(pS0.t[:, 384:512], Bb.t[:, cs], Cb.t[:, cs], True, True, [Bb.r, Cb.r], [r_CB])
                        yield
                        tt(CBm.t[:], pS0.t[:, 384:512], tri_ap, ALU.mult, [r_CB, cst.r], [CBm.r])
                        yield
                        mm(pS2.t[:, 256:512], Cb.t[:, cs], hprev.t[:, gs], True, True, [Cb.r, r_hp[g]], [r_yo])
                        yield
                        for r in range(4):
                            h = g * 4 + r
                            ts(lhD[r].t[:], strict_ap, ac[c].t[:, h:h + 1], ALU.mult, [cst.r, ac[c].r], [lhD[r].r])
                            yield
                        yield True
                        for r in range(4):
                            mm(pS1.t[:, r * 128:(r + 1) * 128], lhD[r].t[:], tri_ap, True, True,
                               [lhD[r].r, cst.r], [r_D[r]])
                        yield
                        for r in range(4):
                            k2 = r % 2
                            act(Eh[k2].t[:], pS1.t[:, r * 128:(r + 1) * 128], AF.Exp, [r_D[r]], [Eh[k2].r])
                            yield
                            tt(scT[k2].t[:], Eh[k2].t[:], CBm.t[:], ALU.mult, [Eh[k2].r, CBm.r], [scT[k2].r])
                            yield
                            mm(pS2.t[:, r * 64:(r + 1) * 64], scT[k2].t[:], xdt.t[:, r * 64:(r + 1) * 64], True, True,
                               [scT[k2].r, xdt.r], [r_yd])
                            yield
                        tt(v3(ysb.t[:]), v3(pS2.t[:, 256:512]), bc4(eacs[c]), ALU.mult, [r_yo, eacs[c].r], [ysb.r])
                        yield
                        tt(ysb.t[:], ysb.t[:], pS2.t[:, 0:256], ALU.add, [ysb.r, r_yd], [ysb.r])
                        yield
                        tt(v3(yt2.t[:]), v3(x_tok.t[:]),
                           hvec.t[:, 64 + g * 4:64 + (g + 1) * 4].unsqueeze(2).to_broadcast([128, 4, 64]),
                           ALU.mult, [x_tok.r, hvec.r], [yt2.r])
                        yield
                        tt(ysb.t[:], ysb.t[:], yt2.t[:], ALU.add, [ysb.r, yt2.r], [ysb.r])
                        yield
                        tt(ysb.t[:], ysb.t[:], zs[c].t[:], ALU.mult, [ysb.r, zs[c].r], [ysb.r])
                        yield
                        act(yt2.t[:], ysb.t[:], AF.Square, [ysb.r], [yt2.r, ssq.r], accum=ssq.t[:])
                        yield
                        act(grs.t[:], ssq.t[:], AF.Ln, [ssq.r], [grs.r], bias=EPS, scale=1.0 / 256)
                        act(grs.t[:], grs.t[:], AF.Exp, [grs.r], [grs.r], scale=-0.5)
                        yield
                        ts(yt2.t[:], ysb.t[:], grs.t[:, 0:1], ALU.mult, [ysb.r, grs.r], [yt2.r])
                        yield
                        yield True
                        for i in range(2):
                            tp(pS3.t[:, 256 + i * 128:256 + (i + 1) * 128], yt2.t[:, i * 128:(i + 1) * 128],
                               [yt2.r], [r_yT])
                        yield
                        for i in range(2):
                            act(mix[16 + 2 * g + i].t[:, cs], pS3.t[:, 256 + i * 128:256 + (i + 1) * 128], AF.Copy,
                                [r_yT, pvec.r], [mix[16 + 2 * g + i].r], scale=pv(PV_SNG + 2 * g + i))
                        yield
                    tt(v3(hstate.t[:, gs]), v3(hstate.t[:, gs]), bc4(decb[c]), ALU.mult, [r_hs[g], decb[c].r], [r_hs[g]])
                    yield
                    tt(hstate.t[:, gs], hstate.t[:, gs], pS3.t[:, 0:256], ALU.add, [r_hs[g], r_st], [r_hs[g]])
                    yield
                    if mode != "state":
                        act(hprev.t[:, gs], hstate.t[:, gs], AF.Copy, [r_hs[g]], [r_hp[g]])
                        yield

            run(prep(0, 0))
            for g in range(NG):
                mains = []
                if g + 1 < NG:
                    mains.append(prep(g + 1, (g + 1) % 2))
                if mode != "state":
                    mains.append(branchA(g))
                weave(chain(*mains), ssd(g, g % 2))
            if not full or KSTOP <= 4:
                return

            for cb in range(8):
                banks = [next_pg() for _ in range(2)]
                for hf in range(4):
                    run(proj_fm(w_out, hf * 1024, 1, cb * 256, 256, mix[hf * 8:(hf + 1) * 8], n, banks=banks,
                                first=(hf == 0), last=(hf == 3)))
                for ci in range(2):
                    act(ob[cb * 2 + ci].t[:, 0:n], banks[ci].t[:, 0:n], AF.Copy, [banks[ci].r], [ob[cb * 2 + ci].r])
            sumsq_rstd([(ob[j].t[:, 0:n], ob[j].r) for j in range(16)], n, D)
            for j in range(16):
                stt(ob[j].t[:, 0:n], ob[j].t[:, 0:n], pv(PV_POSTMIX + j), rstd.t[:, 0:n], ALU.mult, ALU.mult,
                    [ob[j].r, pvec.r, rstd.r], [ob[j].r])
                tt(xb[j].t[:, 0:n], xb[j].t[:, 0:n], ob[j].t[:, 0:n], ALU.add, [xb[j].r, ob[j].r], [xb[j].r])
            sumsq_rstd([(xb[j].t[:, 0:n], xb[j].r) for j in range(16)], n, D)
            for j in range(16):
                stt(hb[j].t[:, 0:n], xb[j].t[:, 0:n], pv(PV_PREMLP + j), rstd.t[:, 0:n], ALU.mult, ALU.mult,
                    [xb[j].r, pvec.r, rstd.r], [hb[j].r])
            for half in range(2):
                for cb in range(16):
                    banks = run(proj_fm(w_ff1, 0, 2, half * 4096 + cb * 256, 256, hb, n))
                    for ci in range(2):
                        m = mix[cb * 2 + ci]
                        act(cvb[ci].t[:, 0:n], banks[ci].t[:, 0:n], AF.Relu, [banks[ci].r], [cvb[ci].r])
                        tt(m.t[:, 0:n], cvb[ci].t[:, 0:n], cvb[ci].t[:, 0:n], ALU.mult, [cvb[ci].r], [m.r])
                for cb in range(8):
                    banks = [next_pg() for _ in range(2)]
                    for hf in range(4):
                        run(proj_fm(w_ff2, half * 4096 + hf * 1024, 1, cb * 256, 256, mix[hf * 8:(hf + 1) * 8], n,
                                    banks=banks, first=(hf == 0), last=(hf == 3)))
                    for ci in range(2):
                        o = ob[cb * 2 + ci]
                        if half == 0:
                            act(o.t[:, 0:n], banks[ci].t[:, 0:n], AF.Copy, [banks[ci].r], [o.r])
                        else:
                            tt(o.t[:, 0:n], o.t[:, 0:n], banks[ci].t[:, 0:n], ALU.add, [o.r, banks[ci].r], [o.r])
            sumsq_rstd([(ob[j].t[:, 0:n], ob[j].r) for j in range(16)], n, D)
            for j in range(16):
                stt(ob[j].t[:, 0:n], ob[j].t[:, 0:n], pv(PV_POSTMLP + j), rstd.t[:, 0:n], ALU.mult, ALU.mult,
                    [ob[j].r, pvec.r, rstd.r], [ob[j].r])
                tt(ob[j].t[:, 0:n], ob[j].t[:, 0:n], xb[j].t[:, 0:n], ALU.add, [ob[j].r, xb[j].r], [ob[j].r])
                S.dma("sp", c_o[j], outT[j * 128:(j + 1) * 128, t0 - PRE - PFX:t0 - PRE - PFX + n], ob[j].t[:, 0:n],
                      reads=[ob[j].r])

        if exchange and ntile >= 0:
            tile(0, PRE, "state", True)
            for ti in range(ntile):
                tile(PRE + ti * NT, NT, "state", False)
            c_ex = S.chan()
            S.dma("sp", c_ex, ex_src[:, 0:DSSM], hstate.t[:], reads=r_hs)
            S.dma("sp", c_ex, ex_src[:, DSSM:XW], ltot.t[:], reads=[ltot.r])
            r_exd = Res()
            c_cc = S.chan()
            S.wait("pool", [(c_ex.key, c_ex.cnt)])
            S.raw("pool", c_cc, lambda e: e.collective_compute(
                "AllGather", ALU.bypass, replica_groups=[list(range(NCORE))],
                ins=[ex_src], outs=[ex_dst]), writes=[r_exd])
            lall = sb("lall", [128, NCORE, NH], F32)
            for j in range(NCORE):
                S.dma("sp", c_ex, lall.t[:, j, :], ex_dst[j * 128:(j + 1) * 128, DSSM:XW], reads=[r_exd],
                      writes=[lall.r])
            coef = sb("coef", [128, NH], F32)
            sj = ob
            S.emit("dve", lambda e: e.memset(hstate.t[:], 0.0), [], r_hs)
            for j in range(NCORE):
                ts(coef.t[:], lall.t[:, 0, :], xmask.t[:, 8 + j * 8:8 + j * 8 + 1], ALU.mult,
                   [lall.r, xmask.r], [coef.r])
                for k in range(1, NCORE):
                    stt(coef.t[:], lall.t[:, k, :], xmask.t[:, 8 + j * 8 + k:8 + j * 8 + k + 1], coef.t[:],
                        ALU.mult, ALU.add, [lall.r, xmask.r, coef.r], [coef.r])
                act(coef.t[:], coef.t[:], AF.Exp, [coef.r], [coef.r])
                ts(coef.t[:], coef.t[:], xmask.t[:, j:j + 1], ALU.mult, [coef.r, xmask.r], [coef.r])
                for qd in range(4):
                    S.dma("sp", c_ex, sj[qd].t[:, 0:512], ex_dst[j * 128:(j + 1) * 128, qd * 512:(qd + 1) * 512],
                          reads=[r_exd], writes=[sj[qd].r])
                    tt(sj[qd].t[:, 0:512].rearrange("p (h j) -> p h j", h=8),
                       sj[qd].t[:, 0:512].rearrange("p (h j) -> p h j", h=8),
                       coef.t[:, qd * 8:(qd + 1) * 8].unsqueeze(2).to_broadcast([128, 8, 64]), ALU.mult,
                       [sj[qd].r, coef.r], [sj[qd].r])
                    tt(hstate.t[:, qd * 512:(qd + 1) * 512], hstate.t[:, qd * 512:(qd + 1) * 512], sj[qd].t[:, 0:512],
                       ALU.add, [r_hs[2 * qd], r_hs[2 * qd + 1], sj[qd].r], [r_hs[2 * qd], r_hs[2 * qd + 1]])
            for g in range(NG):
                act(hprev.t[:, g * 256:(g + 1) * 256], hstate.t[:, g * 256:(g + 1) * 256], AF.Copy, [r_hs[g]], [r_hp[g]])
            S.emit("dve", lambda e: e.memset(halo_cv.t[:], 0.0), [], [halo_cv.r])
            S.emit("dve", lambda e: e.memset(halo_x.t[:], 0.0), [], [halo_x.r])

        if not exchange and ntile >= 0:
            npf = NPFXT if ntile == NTILE else int(os.environ.get("KPFX", "0"))
            for ti in range(NPFXT - npf, NPFXT):
                tile(ti * NT, NT, "state", True)
        if ntile >= 0:
            tile(PFX, PRE, "pre", True)
        for ti in range(ntile):
            tile(PFX + PRE + ti * NT, NT, "full", False)
        S.wait("sp", [(c.key, c.cnt) for c in c_o])
        print("instructions:", S.nins, {e: len(S.q[e]) for e in ENGS}, flush=True)
        S.run(st)
    return nc


_CACHE = {}


def _host_consts():
    k = np.arange(128)
    tri = (k[:, None] <= k[None, :]).astype(np.float32)
    strict = (k[:, None] > k[None, :]).astype(np.float32)
    ones = np.ones((128, 128), np.float32)
    ident = np.eye(128, dtype=np.float32)
    return np.ascontiguousarray(np.concatenate([tri, strict, ones, ident], axis=1))


def _fm(v):
    return np.ascontiguousarray(np.asarray(v, np.float32).reshape(-1, 128).T)


def kernel(x, meta_tokens, w_in, short_conv_w, conv_norm_g, ssm_conv_w, ssm_conv_b,
           dt_bias, a_log, d_skip, ssm_norm_g, w_out, pre_mix_g, post_mix_g,
           pre_mlp_g, post_mlp_g, w_ff1, w_ff2, _exchange=False, _ntile=NTILE):
    x = np.asarray(x, np.float32)
    meta = np.asarray(meta_tokens, np.float32)
    key = (_exchange, _ntile)
    if key not in _CACHE:
        _CACHE[key] = build_program(_exchange, _ntile)
    nc = _CACHE[key]
    pvec = np.concatenate(
        [_fm(pre_mix_g[0]), _fm(post_mix_g[0]), _fm(pre_mlp_g[0]), _fm(post_mlp_g[0]), _fm(conv_norm_g[0])]
        + [_fm(short_conv_w[0, k]) for k in range(3)]
        + [_fm(ssm_conv_w[0, k]) for k in range(4)]
        + [_fm(ssm_conv_b[0]), _fm(ssm_norm_g[0])], axis=1)
    assert pvec.shape == (128, PV_N), pvec.shape
    pvec = np.ascontiguousarray(pvec, dtype=np.float32)
    hvec = np.concatenate([np.asarray(dt_bias[0]), np.asarray(a_log[0]), np.asarray(d_skip[0])]).astype(np.float32)
    cst = _host_consts()
    W_in = np.ascontiguousarray(np.asarray(w_in[0], np.float32))
    W_out = np.ascontiguousarray(np.asarray(w_out[0], np.float32))
    W1 = np.ascontiguousarray(np.asarray(w_ff1[0], np.float32))
    W2 = np.ascontiguousarray(np.asarray(w_ff2[0], np.float32))
    in_maps = []
    for c in range(NCORE):
        b, q = divmod(c, NCORE // NB)
        L = PFX + PRE + TPC
        nreal = (q + 1) * TPC
        xt = np.zeros((D, L), np.float32)
        xt[:, L - nreal:] = x[b, :nreal].T
        xt[:, L - nreal - 16:L - nreal] = meta.T
        tm = np.zeros(PFX + PRE, np.float32)
        tm[L - nreal - 16:] = 1.0
        pm = np.ascontiguousarray(tm.reshape(NMASKC, 128).T)
        sel = np.zeros(8, np.float32)
        M = np.zeros((8, 8), np.float32)
        for j in range(NCORE):
            if j // 4 == b and j < c:
                sel[j] = 1.0
                for k2 in range(j + 1, c):
                    M[j, k2] = 1.0
        xm = np.concatenate([sel, M.reshape(-1)]).astype(np.float32)
        in_maps.append({"xT": xt, "w_in": W_in, "w_out": W_out, "w_ff1": W1, "w_ff2": W2, "pvec": pvec,
                        "hvec": hvec, "cst": cst, "pmask": pm, "xmask": xm})
    res = run_bass_kernel_spmd(nc, in_maps, core_ids=list(range(NCORE)))
    out = np.empty((NB, SEQ, D), np.float32)
    nt = _ntile * NT
    for c in range(NCORE):
        b, q = divmod(c, NCORE // NB)
        out[b, q * TPC:q * TPC + nt] = res.results[c]["outT"][:, :nt].T
    return out
```

```python
import os
import numpy as np
from contextlib import ExitStack
import concourse.bass as bass
import concourse.mybir as mybir
from concourse.bass_utils import run_bass_kernel_spmd

F32 = mybir.dt.float32
BF16 = mybir.dt.bfloat16
ALU = mybir.AluOpType
AF = mybir.ActivationFunctionType

ENGS = ["pe", "act", "dve", "pool", "sp"]
INORDER = tuple(os.environ.get("KINORDER", "pe").split(","))

D = 2048
NB, SEQ = 2, 16384
NCORE = 8
TPC = NB * SEQ // NCORE
PRE = 128
PFX = 3 * TPC
NPFXT = PFX // 512
NMASKC = (PFX + PRE) // 128
NT = 512
NTILE = TPC // NT
DCONV = 2048
DSSM = 2048
NH, HP, NG, NS = 32, 64, 8, 128
DXBC = 4096
O_BG, O_CG, O_V, O_Z, O_XBC, O_DT = 0, 2048, 4096, 6144, 8192, 12288
DIN = 12320
DFF = 8192
EPS = 1e-6
HALO = 3
KSTOP = int(os.environ.get("KSTOP", "99"))
KWG = int(os.environ.get("KWG", "16"))

PV_PREMIX, PV_POSTMIX, PV_PREMLP, PV_POSTMLP, PV_CNG = 0, 16, 32, 48, 64
PV_SCW = 80
PV_SSMW = 128
PV_SSMB = 256
PV_SNG = 288
PV_N = 304


class Res:
    __slots__ = ("name", "lw", "rd", "excl")

    def __init__(self, name="", excl=False):
        self.name = name
        self.lw = None
        self.rd = {}
        self.excl = excl


class Chan:
    __slots__ = ("key", "cnt")

    def __init__(self, key):
        self.key = key
        self.cnt = 0


class Sched:
    def __init__(self, nc):
        self.nc = nc
        self.q = {e: [] for e in ENGS}
        self.cnt = {e: 0 for e in ENGS}
        self.known = {e: {} for e in ENGS}
        self.chans = []
        self.nins = 0

    def chan(self):
        c = Chan("d%d" % len(self.chans))
        self.chans.append(c)
        return c

    def _waits(self, eng, deps):
        best = {}
        for d in deps:
            if d is None:
                continue
            k, v = d
            if k == eng and eng in INORDER:
                continue
            if best.get(k, 0) < v:
                best[k] = v
        waits = []
        kn = self.known[eng]
        for k, v in best.items():
            if kn.get(k, 0) >= v:
                continue
            kn[k] = v
            waits.append((k, v))
        return waits

    def emit(self, eng, fn, reads=(), writes=(), signal=True):
        if any(r.excl for r in reads):
            writes = list(writes) + [r for r in reads if r.excl]
            reads = [r for r in reads if not r.excl]
        deps = []
        for r in reads:
            deps.append(r.lw)
        for w in writes:
            deps.append(w.lw)
            deps.extend(w.rd.values())
        waits = self._waits(eng, deps)
        if signal:
            self.cnt[eng] += 1
            tok = (eng, self.cnt[eng])
        else:
            tok = (eng, self.cnt[eng] + 1)
        for r in reads:
            r.rd[eng] = tok
        for w in writes:
            w.lw = tok
            w.rd = {}
        self.q[eng].append((waits, fn, eng if signal else None, 1))
        self.nins += 1
        return tok

    def dma(self, eng, chan, out, in_, reads=(), writes=()):
        deps = [(chan.key, chan.cnt)] if chan.cnt else []
        for r in reads:
            deps.append(r.lw)
        for w in writes:
            deps.append(w.lw)
            deps.extend(w.rd.values())
        waits = self._waits(eng, deps)
        chan.cnt += 16
        tok = (chan.key, chan.cnt)
        for r in reads:
            r.rd[chan.key] = tok
        for w in writes:
            w.lw = tok
            w.rd = {}
        self.q[eng].append((waits, lambda e: e.dma_start(out=out, in_=in_), chan.key, 16))
        self.nins += 1
        return tok

    def raw(self, eng, chan, fn, reads=(), writes=()):
        deps = [(chan.key, chan.cnt)] if chan.cnt else []
        for r in reads:
            deps.append(r.lw)
        for w in writes:
            deps.append(w.lw)
            deps.extend(w.rd.values())
        waits = self._waits(eng, deps)
        chan.cnt += 16
        tok = (chan.key, chan.cnt)
        for r in reads:
            r.rd[chan.key] = tok
        for w in writes:
            w.lw = tok
            w.rd = {}
        self.q[eng].append((waits, fn, chan.key, 16))
        return tok

    def wait(self, eng, deps):
        waits = self._waits(eng, deps)
        if waits:
            self.q[eng].append((waits, None, None, 0))

    def run(self, stack):
        nc = self.nc
        sems = {}
        for k in ENGS + [c.key for c in self.chans]:
            sems[k] = stack.enter_context(nc.semaphore("s_" + k))
        block = stack.enter_context(nc.Block())

        def body(eng):
            def f(e):
                for waits, fn, sigkey, inc in self.q[eng]:
                    for k, v in waits:
                        e.wait_ge(sems[k], v)
                    if fn is None:
                        continue
                    ins = fn(e)
                    if sigkey is not None:
                        ins.then_inc(sems[sigkey], inc)
            return f

        block.tensor(body("pe"))
        block.scalar(body("act"))
        block.vector(body("dve"))
        block.gpsimd(body("pool"))
        block.sync(body("sp"))


class T:
    __slots__ = ("t", "r")

    def __init__(self, t, name=""):
        self.t = t
        self.r = Res(name)


def build_program(exchange=True, ntile=NTILE):
    nc = bass.Bass("TRN2", target_bir_lowering=False)
    TOT = PFX + PRE + TPC
    xT = nc.dram_tensor("xT", [D, TOT], F32, kind="ExternalInput").ap()
    w_in = nc.dram_tensor("w_in", [D, DIN], F32, kind="ExternalInput").ap()
    w_out = nc.dram_tensor("w_out", [2 * D, D], F32, kind="ExternalInput").ap()
    w_ff1 = nc.dram_tensor("w_ff1", [D, DFF], F32, kind="ExternalInput").ap()
    w_ff2 = nc.dram_tensor("w_ff2", [DFF, D], F32, kind="ExternalInput").ap()
    pvec_d = nc.dram_tensor("pvec", [128, PV_N], F32, kind="ExternalInput").ap()
    hvec_d = nc.dram_tensor("hvec", [96], F32, kind="ExternalInput").ap()
    cst_d = nc.dram_tensor("cst", [128, 512], F32, kind="ExternalInput").ap()
    pmask_d = nc.dram_tensor("pmask", [128, NMASKC], F32, kind="ExternalInput").ap()
    xmask_d = nc.dram_tensor("xmask", [72], F32, kind="ExternalInput").ap()
    outT = nc.dram_tensor("outT", [D, TPC], F32, kind="ExternalOutput").ap()
    XW = DSSM + NH
    if exchange:
        ex_src = nc.dram_tensor("ex_src", [128, XW], F32).ap()
        ex_dst = nc.dram_tensor("ex_dst", [NCORE * 128, XW], F32).ap()

    with ExitStack() as st:
        S = Sched(nc)

        def sb(name, shape, dt):
            return T(st.enter_context(nc.sbuf_tensor("sb_" + name, shape, dt)), name)

        def ps(name, shape, dt):
            t = T(st.enter_context(nc.psum_tensor("ps_" + name, shape, dt)), name)
            t.r.excl = True
            return t

        def mm(out, lhsT, rhs, start, stop, reads, writes, signal=True):
            S.emit("pe", lambda e: e.matmul(out, lhsT=lhsT, rhs=rhs, start=start, stop=stop),
                   reads, writes, signal)

        def tp(out, in_, reads, writes):
            S.emit("pe", lambda e: e.transpose(out, in_, ident_ap), list(reads) + [cst.r], writes)

        def act(out, in_, func, reads, writes, bias=None, scale=None, accum=None):
            kw = {}
            if bias is not None:
                kw["bias"] = bias
            if scale is not None:
                kw["scale"] = scale
            if accum is not None:
                kw["accum_out"] = accum
            S.emit("act", lambda e: e.activation(out=out, in_=in_, func=func, **kw), reads, writes)

        def tt(out, in0, in1, op, reads, writes):
            S.emit("dve", lambda e: e.tensor_tensor(out=out, in0=in0, in1=in1, op=op), reads, writes)

        def ts(out, in0, s1, op0, reads, writes, s2=None, op1=None):
            if op1 is None:
                S.emit("dve", lambda e: e.tensor_scalar(out=out, in0=in0, scalar1=s1, scalar2=None, op0=op0),
                       reads, writes)
            else:
                S.emit("dve", lambda e: e.tensor_scalar(out=out, in0=in0, scalar1=s1, scalar2=s2, op0=op0, op1=op1),
                       reads, writes)

        def stt(out, in0, scalar, in1, op0, op1, reads, writes):
            S.emit("dve", lambda e: e.scalar_tensor_tensor(out=out, in0=in0, scalar=scalar, in1=in1, op0=op0, op1=op1),
                   reads, writes)

        def cpy(out, in_, reads, writes):
            S.emit("dve", lambda e: e.tensor_copy(out=out, in_=in_), reads, writes)

        xb = [sb("xb%d" % j, [128, NT], F32) for j in range(16)]
        ob = [sb("ob%d" % j, [128, NT], F32) for j in range(16)]
        hb = [sb("hb%d" % j, [128, NT], BF16) for j in range(16)]
        mix = [sb("mix%d" % j, [128, NT], BF16) for j in range(32)]
        NSLOT = 6
        wsl = [sb("ws%d" % i, [128, 8, 256], BF16) for i in range(NSLOT)]
        wch = [S.chan() for _ in range(NSLOT)]
        wstate = {"i": 0}
        pg = [ps("pg%d" % i, [128, 512], F32) for i in range(4)]
        pgstate = {"i": 0}
        pS0 = ps("pS0", [128, 512], F32)
        pS1 = ps("pS1", [128, 512], F32)
        pS2 = ps("pS2", [128, 512], F32)
        pS3 = ps("pS3", [128, 512], F32)
        r_xsT = r_BT = r_CB = pS0.r
        r_D = [pS1.r for _ in range(4)]
        r_yd = r_yo = pS2.r
        r_st = r_yT = pS3.r

        cst = sb("cst", [128, 512], F32)
        tri_ap = cst.t[:, 0:128]
        strict_ap = cst.t[:, 128:256]
        ones_ap = cst.t[:, 256:384]
        ident_ap = cst.t[:, 384:512]

        onesb = sb("onesb", [128, 128], BF16)
        pvec = sb("pvec", [128, PV_N], F32)
        hvec = sb("hvec", [128, 96], F32)
        aneg = sb("aneg", [128, 32], F32)
        pmask = sb("pmask", [128, NMASKC], F32)
        xmask = sb("xmask", [128, 72], F32)
        sq = [sb("sq%d" % i, [128, NT], BF16) for i in range(2)]
        rstd = sb("rstd", [128, NT], F32)
        cvb = [sb("cvb%d" % i, [128, HALO + NT], F32) for i in range(2)]
        cacc = [sb("cacc%d" % i, [128, NT], F32) for i in range(2)]
        xcbS = [[sb("xcb%d_%d" % (b_, i), [128, HALO + NT], F32) for i in range(4)] for b_ in range(2)]
        BbS = [sb("Bb%d" % b_, [128, NT], BF16) for b_ in range(2)]
        CbS = [sb("Cb%d" % b_, [128, NT], BF16) for b_ in range(2)]
        halo_cv = sb("halo_cv", [128, 16, HALO], F32)
        halo_x = sb("halo_x", [128, 32, HALO], F32)
        NCH = NT // 128
        zsS = [[sb("zs%d_%d" % (b_, c), [128, 256], BF16) for c in range(NCH)] for b_ in range(2)]
        dtc = [sb("dtc%d" % c, [128, 32], F32) for c in range(NCH)]
        ac = [sb("ac%d" % c, [128, 32], F32) for c in range(NCH)]
        acs = [sb("acs%d" % c, [128, 32], F32) for c in range(NCH)]
        eacs = [sb("eacs%d" % c, [128, 32], F32) for c in range(NCH)]
        dtdte = [sb("dtdte%d" % c, [128, 32], F32) for c in range(NCH)]
        decb = [sb("decb%d" % c, [128, 32], F32) for c in range(NCH)]
        sm1 = sb("sm1", [128, 32], F32)
        sm2 = sb("sm2", [128, 32], F32)
        ltot = sb("ltot", [128, 32], F32)
        x_tok = sb("x_tok", [128, 256], F32)
        xdt = sb("xdt", [128, 256], BF16)
        xdec = sb("xdec", [128, 256], BF16)
        B_tok = sb("B_tok", [128, 128], BF16)
        B_tokS = [B_tok, sb("B_tok1", [128, 128], BF16)]
        xdecS = [xdec, sb("xdec1", [128, 256], BF16)]
        CBm = sb("CBm", [128, 128], F32)
        lhD4 = sb("lhD4", [128, 512], F32)
        Eh4 = sb("Eh4", [128, 512], F32)
        scT4 = sb("scT4", [128, 512], BF16)
        ysb = sb("ysb", [128, 256], F32)
        yt2 = sb("yt2", [128, 256], F32)
        ssq = sb("ssq", [128, 1], F32)
        grs = sb("grs", [128, 1], F32)
        hstate = sb("hstate", [128, DSSM], F32)
        hprev = sb("hprev", [128, DSSM], BF16)
        r_hs = [Res() for _ in range(NG)]
        r_hp = [Res() for _ in range(NG)]

        c_const = S.chan()
        c_x = [S.chan() for _ in range(16)]
        c_o = [S.chan() for _ in range(16)]

        S.dma("sp", c_const, cst.t[:], cst_d, writes=[cst.r])
        S.dma("sp", c_const, pvec.t[:], pvec_d, writes=[pvec.r])
        S.dma("sp", c_const, hvec.t[:], hvec_d.partition_broadcast(128), writes=[hvec.r])
        S.dma("sp", c_const, pmask.t[:], pmask_d, writes=[pmask.r])
        S.dma("sp", c_const, xmask.t[:], xmask_d.partition_broadcast(128), writes=[xmask.r])
        cpy(onesb.t[:], ones_ap, [cst.r], [onesb.r])
        act(aneg.t[:], hvec.t[:, 32:64], AF.Exp, [hvec.r], [aneg.r])
        ts(aneg.t[:], aneg.t[:], -1.0, ALU.mult, [aneg.r], [aneg.r])
        S.emit("dve", lambda e: e.memset(halo_cv.t[:], 0.0), [], [halo_cv.r])
        S.emit("dve", lambda e: e.memset(halo_x.t[:], 0.0), [], [halo_x.r])
        S.emit("dve", lambda e: e.memset(hstate.t[:], 0.0), [], r_hs)
        S.emit("dve", lambda e: e.memset(hprev.t[:], 0.0), [], r_hp)
        S.emit("dve", lambda e: e.memset(ltot.t[:], 0.0), [], [ltot.r])

        def pv(col):
            return pvec.t[:, col:col + 1]

        def wload(src, ncols):
            i = wstate["i"] % NSLOT
            wstate["i"] += 1
            S.dma("pool", wch[i], wsl[i].t[:, :, 0:ncols], src.rearrange("(kc p) n -> p kc n", p=128),
                  writes=[wsl[i].r])
            return wsl[i]

        def next_pg():
            b = pg[pgstate["i"] % 4]
            pgstate["i"] += 1
            return b

        def proj_fm(W, row0, nk, col0, ncols, src, n, banks=None, first=True, last=True):
            nchunk = ncols // 128
            if banks is None:
                banks = [next_pg() for _ in range(nchunk)]
            slots = [wload(W[row0 + h * 1024: row0 + (h + 1) * 1024, col0:col0 + ncols], ncols) for h in range(nk)]
            for ci in range(nchunk):
                for kc in range(nk * 8):
                    sl = slots[kc // 8]
                    mm(banks[ci].t[:, 0:n], sl.t[:, kc % 8, ci * 128:(ci + 1) * 128], src[kc].t[:, 0:n],
                       start=(first and kc == 0), stop=(last and kc == nk * 8 - 1),
                       reads=[sl.r, src[kc].r], writes=[banks[ci].r], signal=(kc % 8 == 7))
                    if kc % KWG == KWG - 1:
                        yield (last and kc == nk * 8 - 1)
            return banks

        def run(gen):
            try:
                while True:
                    next(gen)
            except StopIteration as e:
                return e.value

        def chain(*gens):
            for g_ in gens:
                yield from g_

        def weave(main, side):
            if os.environ.get("KNOWEAVE"):
                run(main)
                run(side)
                return
            ma = sa = True
            mb, sn = True, False
            while ma or sa:
                if ma and not (sa and sn and mb):
                    try:
                        v = next(main)
                        if v is not None:
                            mb = bool(v)
                    except StopIteration:
                        ma, mb = False, True
                if sa and (mb or not sn):
                    try:
                        sn = bool(next(side))
                    except StopIteration:
                        sa, sn = False, False

        def sumsq_rstd(srcs, n, dim, pre=None):
            bank = next_pg()
            for j, (ap, r) in enumerate(srcs):
                s = sq[j % 2]
                act(s.t[:, 0:n], ap, AF.Square, [r], [s.r])
                mm(bank.t[:, 0:n], onesb.t[:], s.t[:, 0:n], start=(j == 0), stop=(j == len(srcs) - 1),
                   reads=[onesb.r, s.r], writes=[bank.r], signal=True)
            act(rstd.t[:, 0:n], bank.t[:, 0:n], AF.Ln, [bank.r], [rstd.r], bias=EPS, scale=1.0 / dim)
            act(rstd.t[:, 0:n], rstd.t[:, 0:n], AF.Exp, [rstd.r], [rstd.r], scale=-0.5)

        def conv_fm(buf, hal, hidx, n, wcols, bias_col, out_ap, out_res, func, save_halo=True):
            K = len(wcols)
            S.emit("act", lambda e: e.activation(out=buf.t[:, 0:HALO], in_=hal.t[:, hidx, :], func=AF.Copy),
                   [hal.r], [buf.r])
            acc = cacc[hidx % 2]
            if bias_col is None:
                ts(acc.t[:, 0:n], buf.t[:, HALO:HALO + n], pv(wcols[K - 1]), ALU.mult, [buf.r, pvec.r], [acc.r])
            else:
                ts(acc.t[:, 0:n], buf.t[:, HALO:HALO + n], pv(wcols[K - 1]), ALU.mult, [buf.r, pvec.r], [acc.r],
                   s2=pv(bias_col), op1=ALU.add)
            for d in range(1, K):
                stt(acc.t[:, 0:n], buf.t[:, HALO - d:HALO - d + n], pv(wcols[K - 1 - d]), acc.t[:, 0:n],
                    ALU.mult, ALU.add, [buf.r, pvec.r, acc.r], [acc.r])
            if save_halo:
                S.emit("act", lambda e: e.activation(out=hal.t[:, hidx, :], in_=buf.t[:, n:n + HALO], func=AF.Copy),
                       [buf.r], [hal.r])
            if func is not None:
                act(out_ap, acc.t[:, 0:n], func, [acc.r], [out_res])
            return acc

        def tile(t0, n, mode, premask):
            nch = n // 128
            full = mode == "full"
            for j in range(16):
                S.dma("sp", c_x[j], xb[j].t[:, 0:n], xT[j * 128:(j + 1) * 128, t0:t0 + n], writes=[xb[j].r])
            sumsq_rstd([(xb[j].t[:, 0:n], xb[j].r) for j in range(16)], n, D)
            for j in range(16):
                stt(hb[j].t[:, 0:n], xb[j].t[:, 0:n], pv(PV_PREMIX + j), rstd.t[:, 0:n], ALU.mult, ALU.mult,
                    [xb[j].r, pvec.r, rstd.r], [hb[j].r])

            dslots = [wload(w_in[h * 1024:(h + 1) * 1024, O_DT:O_DT + 32], 32) for h in range(2)]
            for c in range(nch):
                bank = next_pg()
                for kc in range(16):
                    sl = dslots[kc // 8]
                    mm(bank.t[:, 0:32], hb[kc].t[:, c * 128:(c + 1) * 128], sl.t[:, kc % 8, 0:32],
                       start=(kc == 0), stop=(kc == 15), reads=[sl.r, hb[kc].r], writes=[bank.r],
                       signal=(kc % 8 == 7))
                tt(sm1.t[:], bank.t[:, 0:32], hvec.t[:, 0:32], ALU.add, [bank.r, hvec.r], [sm1.r])
                act(sm1.t[:], sm1.t[:], AF.Exp, [sm1.r], [sm1.r])
                act(dtc[c].t[:], sm1.t[:], AF.Ln, [sm1.r], [dtc[c].r], bias=1.0)
                if premask:
                    gc = t0 // 128 + c
                    ts(dtc[c].t[:], dtc[c].t[:], pmask.t[:, gc:gc + 1], ALU.mult, [dtc[c].r, pmask.r], [dtc[c].r])
                tt(ac[c].t[:], dtc[c].t[:], aneg.t[:], ALU.mult, [dtc[c].r, aneg.r], [ac[c].r])
                b2 = next_pg()
                mm(b2.t[:, 0:32], tri_ap, ac[c].t[:], True, True, [cst.r, ac[c].r], [b2.r])
                mm(b2.t[:, 32:64], ones_ap, ac[c].t[:], True, True, [cst.r, ac[c].r], [b2.r])
                cpy(acs[c].t[:], b2.t[:, 0:32], [b2.r], [acs[c].r])
                act(eacs[c].t[:], b2.t[:, 0:32], AF.Exp, [b2.r], [eacs[c].r])
                act(decb[c].t[:], b2.t[:, 32:64], AF.Exp, [b2.r], [decb[c].r])
                tt(sm2.t[:], b2.t[:, 32:64], acs[c].t[:], ALU.subtract, [b2.r, acs[c].r], [sm2.r])
                act(sm2.t[:], sm2.t[:], AF.Exp, [sm2.r], [sm2.r])
                tt(dtdte[c].t[:], sm2.t[:], dtc[c].t[:], ALU.mult, [sm2.r, dtc[c].r], [dtdte[c].r])

            def branchA(jj):
                PC = yield from proj_fm(w_in, 0, 2, O_CG + jj * 256, 256, hb, n)
                for ci in range(2):
                    act(cacc[ci].t[:, 0:n], PC[ci].t[:, 0:n], AF.Copy, [PC[ci].r], [cacc[ci].r])
                    yield
                PVb = yield from proj_fm(w_in, 0, 2, O_V + jj * 256, 256, hb, n)
                for ci in range(2):
                    j = jj * 2 + ci
                    tt(cvb[ci].t[:, HALO:HALO + n], cacc[ci].t[:, 0:n], PVb[ci].t[:, 0:n], ALU.mult,
                       [cacc[ci].r, PVb[ci].r], [cvb[ci].r])
                    yield
                    if full:
                        conv_fm(cvb[ci], halo_cv, j, n, [PV_SCW + k * 16 + j for k in range(3)], None, None, None, None)
                    else:
                        S.emit("act", (lambda ci, j: lambda e: e.activation(
                            out=halo_cv.t[:, j, :], in_=cvb[ci].t[:, n:n + HALO], func=AF.Copy))(ci, j),
                            [cvb[ci].r], [halo_cv.r])
                    yield
                if not full:
                    return
                PB = yield from proj_fm(w_in, 0, 2, O_BG + jj * 256, 256, hb, n)
                for ci in range(2):
                    j = jj * 2 + ci
                    acc = cacc[ci]
                    tt(acc.t[:, 0:n], acc.t[:, 0:n], PB[ci].t[:, 0:n], ALU.mult, [acc.r, PB[ci].r], [acc.r])
                    yield
                    sumsq_rstd([(acc.t[:, 0:n], acc.r)], n, 128)
                    yield
                    stt(mix[j].t[:, 0:n], acc.t[:, 0:n], pv(PV_CNG + j), rstd.t[:, 0:n], ALU.mult, ALU.mult,
                        [acc.r, pvec.r, rstd.r], [mix[j].r])
                    yield

            def prep(g, bs):
                X = xcbS[bs]
                PX = yield from proj_fm(w_in, 0, 2, O_XBC + g * 256, 256, hb, n)
                for i in range(2):
                    chn = 2 * g + i
                    act(X[i].t[:, HALO:HALO + n], PX[i].t[:, 0:n], AF.Copy, [PX[i].r], [X[i].r])
                    yield
                    conv_fm(X[i], halo_x, chn, n, [PV_SSMW + k * 32 + chn for k in range(4)], PV_SSMB + chn,
                            X[i].t[:, HALO:HALO + n], X[i].r, AF.Silu)
                    yield
                PBm = yield from proj_fm(w_in, 0, 2, O_XBC + 2048 + g * 128, 128, hb, n)
                chn = 16 + g
                act(X[2].t[:, HALO:HALO + n], PBm[0].t[:, 0:n], AF.Copy, [PBm[0].r], [X[2].r])
                yield
                conv_fm(X[2], halo_x, chn, n, [PV_SSMW + k * 32 + chn for k in range(4)], PV_SSMB + chn,
                        X[2].t[:, HALO:HALO + n], X[2].r, AF.Silu)
                yield
                if mode != "state":
                    PCm = yield from proj_fm(w_in, 0, 2, O_XBC + 3072 + g * 128, 128, hb, n)
                    chn = 24 + g
                    act(X[3].t[:, HALO:HALO + n], PCm[0].t[:, 0:n], AF.Copy, [PCm[0].r], [X[3].r])
                    yield
                    conv_fm(X[3], halo_x, chn, n, [PV_SSMW + k * 32 + chn for k in range(4)], PV_SSMB + chn,
                            CbS[bs].t[:, 0:n], CbS[bs].r, AF.Silu)
                    yield
                if full:
                    cpy(BbS[bs].t[:, 0:n], X[2].t[:, HALO:HALO + n], [X[2].r], [BbS[bs].r])
                    yield
                    zsl = [wload(w_in[h * 1024:(h + 1) * 1024, O_Z + g * 256:O_Z + (g + 1) * 256], 256)
                           for h in range(2)]
                    for c in range(nch):
                        bank = next_pg()
                        for kc in range(16):
                            sl = zsl[kc // 8]
                            mm(bank.t[:, 0:256], hb[kc].t[:, c * 128:(c + 1) * 128], sl.t[:, kc % 8, 0:256],
                               start=(kc == 0), stop=(kc == 15), reads=[sl.r, hb[kc].r], writes=[bank.r],
                               signal=(kc % 8 == 7))
                            if kc % KWG == KWG - 1:
                                yield (kc == 15)
                        act(zsS[bs][c].t[:], bank.t[:, 0:256], AF.Silu, [bank.r], [zsS[bs][c].r])
                        yield

            def ssd(g, bs):
                X = xcbS[bs]
                Bb, Cb, zs = BbS[bs], CbS[bs], zsS[bs]
                gs = slice(g * 256, (g + 1) * 256)
                h4 = slice(g * 4, (g + 1) * 4)

                def bc4(tl):
                    return tl.t[:, h4].unsqueeze(2).to_broadcast([128, 4, 64])

                def v3(ap):
                    return ap.rearrange("p (h j) -> p h j", h=4)

                if not full:
                    def tps(c):
                        pA = pS0 if c % 2 == 0 else pS1
                        xc = slice(HALO + c * 128, HALO + (c + 1) * 128)
                        for i in range(2):
                            tp(pA.t[:, i * 128:(i + 1) * 128], X[i].t[:, xc], [X[i].r], [pA.r])
                        tp(pA.t[:, 256:384], X[2].t[:, xc], [X[2].r], [pA.r])
                    yield True
                    tps(0)
                    yield
                    for c in range(nch):
                        pA, pB = (pS0, pS3) if c % 2 == 0 else (pS1, pS2)
                        Btk, xdc = B_tokS[c % 2], xdecS[c % 2]
                        if c + 1 < nch:
                            yield True
                            tps(c + 1)
                            yield
                        act(Btk.t[:], pA.t[:, 256:384], AF.Copy, [pA.r], [Btk.r])
                        yield
                        tt(v3(xdc.t[:]), v3(pA.t[:, 0:256]), bc4(dtdte[c]), ALU.mult, [pA.r, dtdte[c].r], [xdc.r])
                        yield
                        mm(pB.t[:, 0:256], Btk.t[:], xdc.t[:], True, True, [Btk.r, xdc.r], [pB.r])
                        yield
                        tt(v3(hstate.t[:, gs]), v3(hstate.t[:, gs]), bc4(decb[c]), ALU.mult, [r_hs[g], decb[c].r], [r_hs[g]])
                        yield
                        tt(hstate.t[:, gs], hstate.t[:, gs], pB.t[:, 0:256], ALU.add, [r_hs[g], pB.r], [r_hs[g]])
                        yield
                        if mode != "state":
                            act(hprev.t[:, gs], hstate.t[:, gs], AF.Copy, [r_hs[g]], [r_hp[g]])
                            yield
                    return
                for c in range(nch):
                    cs = slice(c * 128, (c + 1) * 128)
                    xc = slice(HALO + c * 128, HALO + (c + 1) * 128)
                    if full:
                        pA, pB, Btk, xdc = pS0, pS3, B_tok, xdec
                    else:
                        pA, pB = (pS0, pS3) if c % 2 == 0 else (pS1, pS2)
                        Btk, xdc = B_tokS[c % 2], xdecS[c % 2]
                    yield True
                    for i in range(2):
                        tp(pA.t[:, i * 128:(i + 1) * 128], X[i].t[:, xc], [X[i].r], [pA.r])
                    tp(pA.t[:, 256:384], X[2].t[:, xc], [X[2].r], [pA.r])
                    yield
                    act(Btk.t[:], pA.t[:, 256:384], AF.Copy, [pA.r], [Btk.r])
                    yield
                    if full:
                        act(x_tok.t[:], pA.t[:, 0:256], AF.Copy, [pA.r], [x_tok.r])
                        yield
                        tt(v3(xdc.t[:]), v3(x_tok.t[:]), bc4(dtdte[c]), ALU.mult, [x_tok.r, dtdte[c].r], [xdc.r])
                    else:
                        tt(v3(xdc.t[:]), v3(pA.t[:, 0:256]), bc4(dtdte[c]), ALU.mult, [pA.r, dtdte[c].r], [xdc.r])
                    yield
                    mm(pB.t[:, 0:256], Btk.t[:], xdc.t[:], True, True, [Btk.r, xdc.r], [pB.r])
                    yield
                    if full:
                        tt(v3(xdt.t[:]), v3(x_tok.t[:]), bc4(dtc[c]), ALU.mult, [x_tok.r, dtc[c].r], [xdt.r])
                        yield
                        mm(pS0.t[:, 384:512], Bb.t[:, cs], Cb.t[:, cs], True, True, [Bb.r, Cb.r], [r_CB])
                        yield
                        tt(CBm.t[:], pS0.t[:, 384:512], tri_ap, ALU.mult, [r_CB, cst.r], [CBm.r])
                        yield
                        mm(pS2.t[:, 256:512], Cb.t[:, cs], hprev.t[:, gs], True, True, [Cb.r, r_hp[g]], [r_yo])
                        yield
                        tt(lhD4.t[:].rearrange("p (h s) -> p h s", h=4),
                           strict_ap.unsqueeze(1).to_broadcast([128, 4, 128]),
                           ac[c].t[:, h4].unsqueeze(2).to_broadcast([128, 4, 128]), ALU.mult,
                           [cst.r, ac[c].r], [lhD4.r])
                        yield True
                        for r in range(4):
                            mm(pS1.t[:, r * 128:(r + 1) * 128], lhD4.t[:, r * 128:(r + 1) * 128], tri_ap, True, True,
                               [lhD4.r, cst.r], [pS1.r])
                        yield
                        act(Eh4.t[:], pS1.t[:], AF.Exp, [pS1.r], [Eh4.r])
                        yield
                        tt(scT4.t[:].rearrange("p (h s) -> p h s", h=4), Eh4.t[:].rearrange("p (h s) -> p h s", h=4),
                           CBm.t[:].unsqueeze(1).to_broadcast([128, 4, 128]), ALU.mult, [Eh4.r, CBm.r], [scT4.r])
                        yield
                        for r in range(4):
                            mm(pS2.t[:, r * 64:(r + 1) * 64], scT4.t[:, r * 128:(r + 1) * 128],
                               xdt.t[:, r * 64:(r + 1) * 64], True, True, [scT4.r, xdt.r], [r_yd])
                        yield
                        tt(v3(ysb.t[:]), v3(pS2.t[:, 256:512]), bc4(eacs[c]), ALU.mult, [r_yo, eacs[c].r], [ysb.r])
                        yield
                        tt(ysb.t[:], ysb.t[:], pS2.t[:, 0:256], ALU.add, [ysb.r, r_yd], [ysb.r])
                        yield
                        tt(v3(yt2.t[:]), v3(x_tok.t[:]),
                           hvec.t[:, 64 + g * 4:64 + (g + 1) * 4].unsqueeze(2).to_broadcast([128, 4, 64]),
                           ALU.mult, [x_tok.r, hvec.r], [yt2.r])
                        yield
                        tt(ysb.t[:], ysb.t[:], yt2.t[:], ALU.add, [ysb.r, yt2.r], [ysb.r])
                        yield
                        tt(ysb.t[:], ysb.t[:], zs[c].t[:], ALU.mult, [ysb.r, zs[c].r], [ysb.r])
                        yield
                        act(yt2.t[:], ysb.t[:], AF.Square, [ysb.r], [yt2.r, ssq.r], accum=ssq.t[:])
                        yield
                        act(grs.t[:], ssq.t[:], AF.Ln, [ssq.r], [grs.r], bias=EPS, scale=1.0 / 256)
                        act(grs.t[:], grs.t[:], AF.Exp, [grs.r], [grs.r], scale=-0.5)
                        yield
                        ts(yt2.t[:], ysb.t[:], grs.t[:, 0:1], ALU.mult, [ysb.r, grs.r], [yt2.r])
                        yield
                        yield True
                        for i in range(2):
                            tp(pS3.t[:, 256 + i * 128:256 + (i + 1) * 128], yt2.t[:, i * 128:(i + 1) * 128],
                               [yt2.r], [r_yT])
                        yield
                        for i in range(2):
                            act(mix[16 + 2 * g + i].t[:, cs], pS3.t[:, 256 + i * 128:256 + (i + 1) * 128], AF.Copy,
                                [r_yT, pvec.r], [mix[16 + 2 * g + i].r], scale=pv(PV_SNG + 2 * g + i))
                        yield
                    tt(v3(hstate.t[:, gs]), v3(hstate.t[:, gs]), bc4(decb[c]), ALU.mult, [r_hs[g], decb[c].r], [r_hs[g]])
                    yield
                    tt(hstate.t[:, gs], hstate.t[:, gs], pB.t[:, 0:256], ALU.add, [r_hs[g], pB.r], [r_hs[g]])
                    yield
                    if mode != "state":
                        act(hprev.t[:, gs], hstate.t[:, gs], AF.Copy, [r_hs[g]], [r_hp[g]])
                        yield

            run(prep(0, 0))
            for g in range(NG):
                mains = []
                if g + 1 < NG:
                    mains.append(prep(g + 1, (g + 1) % 2))
                if mode != "state":
                    mains.append(branchA(g))
                weave(chain(*mains), ssd(g, g % 2))
            if not full or KSTOP <= 4:
                return

            for cb in range(8):
                banks = [next_pg() for _ in range(2)]
                for hf in range(4):
                    run(proj_fm(w_out, hf * 1024, 1, cb * 256, 256, mix[hf * 8:(hf + 1) * 8], n, banks=banks,
                                first=(hf == 0), last=(hf == 3)))
                for ci in range(2):
                    act(ob[cb * 2 + ci].t[:, 0:n], banks[ci].t[:, 0:n], AF.Copy, [banks[ci].r], [ob[cb * 2 + ci].r])
            sumsq_rstd([(ob[j].t[:, 0:n], ob[j].r) for j in range(16)], n, D)
            for j in range(16):
                stt(ob[j].t[:, 0:n], ob[j].t[:, 0:n], pv(PV_POSTMIX + j), rstd.t[:, 0:n], ALU.mult, ALU.mult,
                    [ob[j].r, pvec.r, rstd.r], [ob[j].r])
                tt(xb[j].t[:, 0:n], xb[j].t[:, 0:n], ob[j].t[:, 0:n], ALU.add, [xb[j].r, ob[j].r], [xb[j].r])
            sumsq_rstd([(xb[j].t[:, 0:n], xb[j].r) for j in range(16)], n, D)
            for j in range(16):
                stt(hb[j].t[:, 0:n], xb[j].t[:, 0:n], pv(PV_PREMLP + j), rstd.t[:, 0:n], ALU.mult, ALU.mult,
                    [xb[j].r, pvec.r, rstd.r], [hb[j].r])
            for half in range(2):
                for cb in range(16):
                    banks = run(proj_fm(w_ff1, 0, 2, half * 4096 + cb * 256, 256, hb, n))
                    for ci in range(2):
                        m = mix[cb * 2 + ci]
                        act(cvb[ci].t[:, 0:n], banks[ci].t[:, 0:n], AF.Relu, [banks[ci].r], [cvb[ci].r])
                        tt(m.t[:, 0:n], cvb[ci].t[:, 0:n], cvb[ci].t[:, 0:n], ALU.mult, [cvb[ci].r], [m.r])
                for cb in range(8):
                    banks = [next_pg() for _ in range(2)]
                    for hf in range(4):
                        run(proj_fm(w_ff2, half * 4096 + hf * 1024, 1, cb * 256, 256, mix[hf * 8:(hf + 1) * 8], n,
                                    banks=banks, first=(hf == 0), last=(hf == 3)))
                    for ci in range(2):
                        o = ob[cb * 2 + ci]
                        if half == 0:
                            act(o.t[:, 0:n], banks[ci].t[:, 0:n], AF.Copy, [banks[ci].r], [o.r])
                        else:
                            tt(o.t[:, 0:n], o.t[:, 0:n], banks[ci].t[:, 0:n], ALU.add, [o.r, banks[ci].r], [o.r])
            sumsq_rstd([(ob[j].t[:, 0:n], ob[j].r) for j in range(16)], n, D)
            for j in range(16):
                stt(ob[j].t[:, 0:n], ob[j].t[:, 0:n], pv(PV_POSTMLP + j), rstd.t[:, 0:n], ALU.mult, ALU.mult,
                    [ob[j].r, pvec.r, rstd.r], [ob[j].r])
                tt(ob[j].t[:, 0:n], ob[j].t[:, 0:n], xb[j].t[:, 0:n], ALU.add, [ob[j].r, xb[j].r], [ob[j].r])
                S.dma("sp", c_o[j], outT[j * 128:(j + 1) * 128, t0 - PRE - PFX:t0 - PRE - PFX + n], ob[j].t[:, 0:n],
                      reads=[ob[j].r])

        if exchange and ntile >= 0:
            tile(0, PRE, "state", True)
            for ti in range(ntile):
                tile(PRE + ti * NT, NT, "state", False)
            c_ex = S.chan()
            S.dma("sp", c_ex, ex_src[:, 0:DSSM], hstate.t[:], reads=r_hs)
            S.dma("sp", c_ex, ex_src[:, DSSM:XW], ltot.t[:], reads=[ltot.r])
            r_exd = Res()
            c_cc = S.chan()
            S.wait("pool", [(c_ex.key, c_ex.cnt)])
            S.raw("pool", c_cc, lambda e: e.collective_compute(
                "AllGather", ALU.bypass, replica_groups=[list(range(NCORE))],
                ins=[ex_src], outs=[ex_dst]), writes=[r_exd])
            lall = sb("lall", [128, NCORE, NH], F32)
            for j in range(NCORE):
                S.dma("sp", c_ex, lall.t[:, j, :], ex_dst[j * 128:(j + 1) * 128, DSSM:XW], reads=[r_exd],
                      writes=[lall.r])
            coef = sb("coef", [128, NH], F32)
            sj = ob
            S.emit("dve", lambda e: e.memset(hstate.t[:], 0.0), [], r_hs)
            for j in range(NCORE):
                ts(coef.t[:], lall.t[:, 0, :], xmask.t[:, 8 + j * 8:8 + j * 8 + 1], ALU.mult,
                   [lall.r, xmask.r], [coef.r])
                for k in range(1, NCORE):
                    stt(coef.t[:], lall.t[:, k, :], xmask.t[:, 8 + j * 8 + k:8 + j * 8 + k + 1], coef.t[:],
                        ALU.mult, ALU.add, [lall.r, xmask.r, coef.r], [coef.r])
                act(coef.t[:], coef.t[:], AF.Exp, [coef.r], [coef.r])
                ts(coef.t[:], coef.t[:], xmask.t[:, j:j + 1], ALU.mult, [coef.r, xmask.r], [coef.r])
                for qd in range(4):
                    S.dma("sp", c_ex, sj[qd].t[:, 0:512], ex_dst[j * 128:(j + 1) * 128, qd * 512:(qd + 1) * 512],
                          reads=[r_exd], writes=[sj[qd].r])
                    tt(sj[qd].t[:, 0:512].rearrange("p (h j) -> p h j", h=8),
                       sj[qd].t[:, 0:512].rearrange("p (h j) -> p h j", h=8),
                       coef.t[:, qd * 8:(qd + 1) * 8].unsqueeze(2).to_broadcast([128, 8, 64]), ALU.mult,
                       [sj[qd].r, coef.r], [sj[qd].r])
                    tt(hstate.t[:, qd * 512:(qd + 1) * 512], hstate.t[:, qd * 512:(qd + 1) * 512], sj[qd].t[:, 0:512],
                       ALU.add, [r_hs[2 * qd], r_hs[2 * qd + 1], sj[qd].r], [r_hs[2 * qd], r_hs[2 * qd + 1]])
            for g in range(NG):
                act(hprev.t[:, g * 256:(g + 1) * 256], hstate.t[:, g * 256:(g + 1) * 256], AF.Copy, [r_hs[g]], [r_hp[g]])
            S.emit("dve", lambda e: e.memset(halo_cv.t[:], 0.0), [], [halo_cv.r])
            S.emit("dve", lambda e: e.memset(halo_x.t[:], 0.0), [], [halo_x.r])

        if not exchange and ntile >= 0:
            npf = NPFXT if ntile == NTILE else int(os.environ.get("KPFX", "0"))
            for ti in range(NPFXT - npf, NPFXT):
                tile(ti * NT, NT, "state", True)
        if ntile >= 0:
            tile(PFX, PRE, "pre", True)
        for ti in range(ntile):
            tile(PFX + PRE + ti * NT, NT, "full", False)
        S.wait("sp", [(c.key, c.cnt) for c in c_o])
        print("instructions:", S.nins, {e: len(S.q[e]) for e in ENGS}, flush=True)
        S.run(st)
    return nc


_CACHE = {}


def _host_consts():
    k = np.arange(128)
    tri = (k[:, None] <= k[None, :]).astype(np.float32)
    strict = (k[:, None] > k[None, :]).astype(np.float32)
    ones = np.ones((128, 128), np.float32)
    ident = np.eye(128, dtype=np.float32)
    return np.ascontiguousarray(np.concatenate([tri, strict, ones, ident], axis=1))


def _fm(v):
    return np.ascontiguousarray(np.asarray(v, np.float32).reshape(-1, 128).T)


def kernel(x, meta_tokens, w_in, short_conv_w, conv_norm_g, ssm_conv_w, ssm_conv_b,
           dt_bias, a_log, d_skip, ssm_norm_g, w_out, pre_mix_g, post_mix_g,
           pre_mlp_g, post_mlp_g, w_ff1, w_ff2, _exchange=False, _ntile=NTILE):
    x = np.asarray(x, np.float32)
    meta = np.asarray(meta_tokens, np.float32)
    key = (_exchange, _ntile)
    if key not in _CACHE:
        _CACHE[key] = build_program(_exchange, _ntile)
    nc = _CACHE[key]
    pvec = np.concatenate(
        [_fm(pre_mix_g[0]), _fm(post_mix_g[0]), _fm(pre_mlp_g[0]), _fm(post_mlp_g[0]), _fm(conv_norm_g[0])]
        + [_fm(short_conv_w[0, k]) for k in range(3)]
        + [_fm(ssm_conv_w[0, k]) for k in range(4)]
        + [_fm(ssm_conv_b[0]), _fm(ssm_norm_g[0])], axis=1)
    assert pvec.shape == (128, PV_N), pvec.shape
    pvec = np.ascontiguousarray(pvec, dtype=np.float32)
    hvec = np.concatenate([np.asarray(dt_bias[0]), np.asarray(a_log[0]), np.asarray(d_skip[0])]).astype(np.float32)
    cst = _host_consts()
    W_in = np.ascontiguousarray(np.asarray(w_in[0], np.float32))
    W_out = np.ascontiguousarray(np.asarray(w_out[0], np.float32))
    W1 = np.ascontiguousarray(np.asarray(w_ff1[0], np.float32))
    W2 = np.ascontiguousarray(np.asarray(w_ff2[0], np.float32))
    in_maps = []
    for c in range(NCORE):
        b, q = divmod(c, NCORE // NB)
        L = PFX + PRE + TPC
        nreal = (q + 1) * TPC
        xt = np.zeros((D, L), np.float32)
        xt[:, L - nreal:] = x[b, :nreal].T
        xt[:, L - nreal - 16:L - nreal] = meta.T
        tm = np.zeros(PFX + PRE, np.float32)
        tm[L - nreal - 16:] = 1.0
        pm = np.ascontiguousarray(tm.reshape(NMASKC, 128).T)
        sel = np.zeros(8, np.float32)
        M = np.zeros((8, 8), np.float32)
        for j in range(NCORE):
            if j // 4 == b and j < c:
                sel[j] = 1.0
                for k2 in range(j + 1, c):
                    M[j, k2] = 1.0
        xm = np.concatenate([sel, M.reshape(-1)]).astype(np.float32)
        in_maps.append({"xT": xt, "w_in": W_in, "w_out": W_out, "w_ff1": W1, "w_ff2": W2, "pvec": pvec,
                        "hvec": hvec, "cst": cst, "pmask": pm, "xmask": xm})
    res = run_bass_kernel_spmd(nc, in_maps, core_ids=list(range(NCORE)))
    out = np.empty((NB, SEQ, D), np.float32)
    nt = _ntile * NT
    for c in range(NCORE):
        b, q = divmod(c, NCORE // NB)
        out[b, q * TPC:q * TPC + nt] = res.results[c]["outT"][:, :nt].T
    return out
```

```python
import os
import numpy as np
from contextlib import ExitStack
import concourse.bass as bass
import concourse.mybir as mybir
from concourse.bass_utils import run_bass_kernel_spmd

F32 = mybir.dt.float32
BF16 = mybir.dt.bfloat16
ALU = mybir.AluOpType
AF = mybir.ActivationFunctionType

ENGS = ["pe", "act", "dve", "pool", "sp"]
INORDER = tuple(os.environ.get("KINORDER", "pe").split(","))

D = 2048
NB, SEQ = 2, 16384
NCORE = 8
TPC = NB * SEQ // NCORE
PRE = 128
PFX = 3 * TPC
NPFXT = PFX // 512
NMASKC = (PFX + PRE) // 128
NT = 512
NTILE = TPC // NT
DCONV = 2048
DSSM = 2048
NH, HP, NG, NS = 32, 64, 8, 128
DXBC = 4096
O_BG, O_CG, O_V, O_Z, O_XBC, O_DT = 0, 2048, 4096, 6144, 8192, 12288
DIN = 12320
DFF = 8192
EPS = 1e-6
HALO = 3
KSTOP = int(os.environ.get("KSTOP", "99"))
KWG = int(os.environ.get("KWG", "16"))

PV_PREMIX, PV_POSTMIX, PV_PREMLP, PV_POSTMLP, PV_CNG = 0, 16, 32, 48, 64
PV_SCW = 80
PV_SSMW = 128
PV_SSMB = 256
PV_SNG = 288
PV_N = 304


class Res:
    __slots__ = ("name", "lw", "rd", "excl")

    def __init__(self, name="", excl=False):
        self.name = name
        self.lw = None
        self.rd = {}
        self.excl = excl


class Chan:
    __slots__ = ("key", "cnt")

    def __init__(self, key):
        self.key = key
        self.cnt = 0


class Sched:
    def __init__(self, nc):
        self.nc = nc
        self.q = {e: [] for e in ENGS}
        self.cnt = {e: 0 for e in ENGS}
        self.known = {e: {} for e in ENGS}
        self.chans = []
        self.nins = 0

    def chan(self):
        c = Chan("d%d" % len(self.chans))
        self.chans.append(c)
        return c

    def _waits(self, eng, deps):
        best = {}
        for d in deps:
            if d is None:
                continue
            k, v = d
            if k == eng and eng in INORDER:
                continue
            if best.get(k, 0) < v:
                best[k] = v
        waits = []
        kn = self.known[eng]
        for k, v in best.items():
            if kn.get(k, 0) >= v:
                continue
            kn[k] = v
            waits.append((k, v))
        return waits

    def emit(self, eng, fn, reads=(), writes=(), signal=True):
        if any(r.excl for r in reads):
            writes = list(writes) + [r for r in reads if r.excl]
            reads = [r for r in reads if not r.excl]
        deps = []
        for r in reads:
            deps.append(r.lw)
        for w in writes:
            deps.append(w.lw)
            deps.extend(w.rd.values())
        waits = self._waits(eng, deps)
        if signal:
            self.cnt[eng] += 1
            tok = (eng, self.cnt[eng])
        else:
            tok = (eng, self.cnt[eng] + 1)
        for r in reads:
            r.rd[eng] = tok
        for w in writes:
            w.lw = tok
            w.rd = {}
        self.q[eng].append((waits, fn, eng if signal else None, 1))
        self.nins += 1
        return tok

    def dma(self, eng, chan, out, in_, reads=(), writes=()):
        deps = [(chan.key, chan.cnt)] if chan.cnt else []
        for r in reads:
            deps.append(r.lw)
        for w in writes:
            deps.append(w.lw)
            deps.extend(w.rd.values())
        waits = self._waits(eng, deps)
        chan.cnt += 16
        tok = (chan.key, chan.cnt)
        for r in reads:
            r.rd[chan.key] = tok
        for w in writes:
            w.lw = tok
            w.rd = {}
        self.q[eng].append((waits, lambda e: e.dma_start(out=out, in_=in_), chan.key, 16))
        self.nins += 1
        return tok

    def raw(self, eng, chan, fn, reads=(), writes=()):
        deps = [(chan.key, chan.cnt)] if chan.cnt else []
        for r in reads:
            deps.append(r.lw)
        for w in writes:
            deps.append(w.lw)
            deps.extend(w.rd.values())
        waits = self._waits(eng, deps)
        chan.cnt += 16
        tok = (chan.key, chan.cnt)
        for r in reads:
            r.rd[chan.key] = tok
        for w in writes:
            w.lw = tok
            w.rd = {}
        self.q[eng].append((waits, fn, chan.key, 16))
        return tok

    def wait(self, eng, deps):
        waits = self._waits(eng, deps)
        if waits:
            self.q[eng].append((waits, None, None, 0))

    def run(self, stack):
        nc = self.nc
        sems = {}
        for k in ENGS + [c.key for c in self.chans]:
            sems[k] = stack.enter_context(nc.semaphore("s_" + k))
        block = stack.enter_context(nc.Block())

        def body(eng):
            def f(e):
                for waits, fn, sigkey, inc in self.q[eng]:
                    for k, v in waits:
                        e.wait_ge(sems[k], v)
                    if fn is None:
                        continue
                    ins = fn(e)
                    if sigkey is not None:
                        ins.then_inc(sems[sigkey], inc)
            return f

        block.tensor(body("pe"))
        block.scalar(body("act"))
        block.vector(body("dve"))
        block.gpsimd(body("pool"))
        block.sync(body("sp"))


class T:
    __slots__ = ("t", "r")

    def __init__(self, t, name=""):
        self.t = t
        self.r = Res(name)


def build_program(exchange=True, ntile=NTILE):
    nc = bass.Bass("TRN2", target_bir_lowering=False)
    TOT = PFX + PRE + TPC
    xT = nc.dram_tensor("xT", [D, TOT], F32, kind="ExternalInput").ap()
    w_in = nc.dram_tensor("w_in", [D, DIN], F32, kind="ExternalInput").ap()
    w_out = nc.dram_tensor("w_out", [2 * D, D], F32, kind="ExternalInput").ap()
    w_ff1 = nc.dram_tensor("w_ff1", [D, DFF], F32, kind="ExternalInput").ap()
    w_ff2 = nc.dram_tensor("w_ff2", [DFF, D], F32, kind="ExternalInput").ap()
    pvec_d = nc.dram_tensor("pvec", [128, PV_N], F32, kind="ExternalInput").ap()
    hvec_d = nc.dram_tensor("hvec", [96], F32, kind="ExternalInput").ap()
    cst_d = nc.dram_tensor("cst", [128, 512], F32, kind="ExternalInput").ap()
    pmask_d = nc.dram_tensor("pmask", [128, NMASKC], F32, kind="ExternalInput").ap()
    xmask_d = nc.dram_tensor("xmask", [72], F32, kind="ExternalInput").ap()
    outT = nc.dram_tensor("outT", [D, TPC], F32, kind="ExternalOutput").ap()
    XW = DSSM + NH
    if exchange:
        ex_src = nc.dram_tensor("ex_src", [128, XW], F32).ap()
        ex_dst = nc.dram_tensor("ex_dst", [NCORE * 128, XW], F32).ap()

    with ExitStack() as st:
        S = Sched(nc)

        def sb(name, shape, dt):
            return T(st.enter_context(nc.sbuf_tensor("sb_" + name, shape, dt)), name)

        def ps(name, shape, dt):
            t = T(st.enter_context(nc.psum_tensor("ps_" + name, shape, dt)), name)
            t.r.excl = True
            return t

        def mm(out, lhsT, rhs, start, stop, reads, writes, signal=True):
            S.emit("pe", lambda e: e.matmul(out, lhsT=lhsT, rhs=rhs, start=start, stop=stop),
                   reads, writes, signal)

        def tp(out, in_, reads, writes):
            S.emit("pe", lambda e: e.transpose(out, in_, ident_ap), list(reads) + [cst.r], writes)

        def act(out, in_, func, reads, writes, bias=None, scale=None, accum=None):
            kw = {}
            if bias is not None:
                kw["bias"] = bias
            if scale is not None:
                kw["scale"] = scale
            if accum is not None:
                kw["accum_out"] = accum
            S.emit("act", lambda e: e.activation(out=out, in_=in_, func=func, **kw), reads, writes)

        def tt(out, in0, in1, op, reads, writes):
            S.emit("dve", lambda e: e.tensor_tensor(out=out, in0=in0, in1=in1, op=op), reads, writes)

        def ts(out, in0, s1, op0, reads, writes, s2=None, op1=None):
            if op1 is None:
                S.emit("dve", lambda e: e.tensor_scalar(out=out, in0=in0, scalar1=s1, scalar2=None, op0=op0),
                       reads, writes)
            else:
                S.emit("dve", lambda e: e.tensor_scalar(out=out, in0=in0, scalar1=s1, scalar2=s2, op0=op0, op1=op1),
                       reads, writes)

        def stt(out, in0, scalar, in1, op0, op1, reads, writes):
            S.emit("dve", lambda e: e.scalar_tensor_tensor(out=out, in0=in0, scalar=scalar, in1=in1, op0=op0, op1=op1),
                   reads, writes)

        def cpy(out, in_, reads, writes):
            S.emit("dve", lambda e: e.tensor_copy(out=out, in_=in_), reads, writes)

        xb = [sb("xb%d" % j, [128, NT], F32) for j in range(16)]
        ob = [sb("ob%d" % j, [128, NT], F32) for j in range(16)]
        hb = [sb("hb%d" % j, [128, NT], BF16) for j in range(16)]
        mix = [sb("mix%d" % j, [128, NT], BF16) for j in range(32)]
        NSLOT = 7
        wsl = [sb("ws%d" % i, [128, 8, 256], BF16) for i in range(NSLOT)]
        wch = [S.chan() for _ in range(NSLOT)]
        wstate = {"i": 0}
        pg = [ps("pg%d" % i, [128, 512], F32) for i in range(4)]
        pgstate = {"i": 0}
        pS0 = ps("pS0", [128, 512], F32)
        pS1 = ps("pS1", [128, 512], F32)
        pS2 = ps("pS2", [128, 512], F32)
        pS3 = ps("pS3", [128, 512], F32)
        r_xsT = r_BT = r_CB = pS0.r
        r_D = [pS1.r for _ in range(4)]
        r_yd = r_yo = pS2.r
        r_st = r_yT = pS3.r

        cst = sb("cst", [128, 512], F32)
        tri_ap = cst.t[:, 0:128]
        strict_ap = cst.t[:, 128:256]
        ones_ap = cst.t[:, 256:384]
        ident_ap = cst.t[:, 384:512]

        onesb = sb("onesb", [128, 128], BF16)
        pvec = sb("pvec", [128, PV_N], F32)
        hvec = sb("hvec", [128, 96], F32)
        aneg = sb("aneg", [128, 32], F32)
        pmask = sb("pmask", [128, NMASKC], F32)
        xmask = sb("xmask", [128, 72], F32)
        sq = [sb("sq%d" % i, [128, NT], BF16) for i in range(2)]
        rstd = sb("rstd", [128, NT], F32)
        cvb = [sb("cvb%d" % i, [128, HALO + NT], F32) for i in range(2)]
        cacc = [sb("cacc%d" % i, [128, NT], F32) for i in range(2)]
        xcbS = [[sb("xcb%d_%d" % (b_, i), [128, HALO + NT], F32) for i in range(4)] for b_ in range(2)]
        BbS = [sb("Bb%d" % b_, [128, NT], BF16) for b_ in range(2)]
        CbS = [sb("Cb%d" % b_, [128, NT], BF16) for b_ in range(2)]
        halo_cv = sb("halo_cv", [128, 16, HALO], F32)
        halo_x = sb("halo_x", [128, 32, HALO], F32)
        NCH = NT // 128
        zsS = [[sb("zs%d_%d" % (b_, c), [128, 256], BF16) for c in range(NCH)] for b_ in range(2)]
        dtc = [sb("dtc%d" % c, [128, 32], F32) for c in range(NCH)]
        ac = [sb("ac%d" % c, [128, 32], F32) for c in range(NCH)]
        acs = [sb("acs%d" % c, [128, 32], F32) for c in range(NCH)]
        eacs = [sb("eacs%d" % c, [128, 32], F32) for c in range(NCH)]
        dtdte = [sb("dtdte%d" % c, [128, 32], F32) for c in range(NCH)]
        decb = [sb("decb%d" % c, [128, 32], F32) for c in range(NCH)]
        sm1 = sb("sm1", [128, 32], F32)
        sm2 = sb("sm2", [128, 32], F32)
        ltot = sb("ltot", [128, 32], F32)
        x_tok = sb("x_tok", [128, 256], F32)
        xdt = sb("xdt", [128, 256], BF16)
        xdec = sb("xdec", [128, 256], BF16)
        B_tok = sb("B_tok", [128, 128], BF16)
        B_tokS = [B_tok, sb("B_tok1", [128, 128], BF16)]
        xdecS = [xdec, sb("xdec1", [128, 256], BF16)]
        CBm = sb("CBm", [128, 128], F32)
        lhD4 = sb("lhD4", [128, 512], F32)
        Eh4 = sb("Eh4", [128, 512], F32)
        scT4 = sb("scT4", [128, 512], BF16)
        ysb = sb("ysb", [128, 256], F32)
        yt2 = sb("yt2", [128, 256], F32)
        ssq = sb("ssq", [128, 1], F32)
        grs = sb("grs", [128, 1], F32)
        hstate = sb("hstate", [128, DSSM], F32)
        hprev = sb("hprev", [128, DSSM], BF16)
        r_hs = [Res() for _ in range(NG)]
        r_hp = [Res() for _ in range(NG)]

        c_const = S.chan()
        c_x = [S.chan() for _ in range(16)]
        c_o = [S.chan() for _ in range(16)]

        S.dma("sp", c_const, cst.t[:], cst_d, writes=[cst.r])
        S.dma("sp", c_const, pvec.t[:], pvec_d, writes=[pvec.r])
        S.dma("sp", c_const, hvec.t[:], hvec_d.partition_broadcast(128), writes=[hvec.r])
        S.dma("sp", c_const, pmask.t[:], pmask_d, writes=[pmask.r])
        S.dma("sp", c_const, xmask.t[:], xmask_d.partition_broadcast(128), writes=[xmask.r])
        cpy(onesb.t[:], ones_ap, [cst.r], [onesb.r])
        act(aneg.t[:], hvec.t[:, 32:64], AF.Exp, [hvec.r], [aneg.r])
        ts(aneg.t[:], aneg.t[:], -1.0, ALU.mult, [aneg.r], [aneg.r])
        S.emit("dve", lambda e: e.memset(halo_cv.t[:], 0.0), [], [halo_cv.r])
        S.emit("dve", lambda e: e.memset(halo_x.t[:], 0.0), [], [halo_x.r])
        S.emit("dve", lambda e: e.memset(hstate.t[:], 0.0), [], r_hs)
        S.emit("dve", lambda e: e.memset(hprev.t[:], 0.0), [], r_hp)
        S.emit("dve", lambda e: e.memset(ltot.t[:], 0.0), [], [ltot.r])

        def pv(col):
            return pvec.t[:, col:col + 1]

        def wload(src, ncols):
            i = wstate["i"] % NSLOT
            wstate["i"] += 1
            S.dma("pool", wch[i], wsl[i].t[:, :, 0:ncols], src.rearrange("(kc p) n -> p kc n", p=128),
                  writes=[wsl[i].r])
            return wsl[i]

        def next_pg():
            b = pg[pgstate["i"] % 4]
            pgstate["i"] += 1
            return b

        def proj_fm(W, row0, nk, col0, ncols, src, n, banks=None, first=True, last=True):
            nchunk = ncols // 128
            if banks is None:
                banks = [next_pg() for _ in range(nchunk)]
            slots = [wload(W[row0 + h * 1024: row0 + (h + 1) * 1024, col0:col0 + ncols], ncols) for h in range(nk)]
            for ci in range(nchunk):
                for kc in range(nk * 8):
                    sl = slots[kc // 8]
                    mm(banks[ci].t[:, 0:n], sl.t[:, kc % 8, ci * 128:(ci + 1) * 128], src[kc].t[:, 0:n],
                       start=(first and kc == 0), stop=(last and kc == nk * 8 - 1),
                       reads=[sl.r, src[kc].r], writes=[banks[ci].r], signal=(kc % 8 == 7))
                    if kc % KWG == KWG - 1:
                        yield (last and kc == nk * 8 - 1)
            return banks

        def run(gen):
            try:
                while True:
                    next(gen)
            except StopIteration as e:
                return e.value

        def chain(*gens):
            for g_ in gens:
                yield from g_

        def weave(main, side):
            if os.environ.get("KNOWEAVE"):
                run(main)
                run(side)
                return
            ma = sa = True
            mb, sn = True, False
            while ma or sa:
                if ma and not (sa and sn and mb):
                    try:
                        v = next(main)
                        if v is not None:
                            mb = bool(v)
                    except StopIteration:
                        ma, mb = False, True
                if sa and (mb or not sn):
                    try:
                        sn = bool(next(side))
                    except StopIteration:
                        sa, sn = False, False

        def sumsq_rstd(srcs, n, dim, pre=None):
            bank = next_pg()
            for j, (ap, r) in enumerate(srcs):
                s = sq[j % 2]
                act(s.t[:, 0:n], ap, AF.Square, [r], [s.r])
                mm(bank.t[:, 0:n], onesb.t[:], s.t[:, 0:n], start=(j == 0), stop=(j == len(srcs) - 1),
                   reads=[onesb.r, s.r], writes=[bank.r], signal=True)
            act(rstd.t[:, 0:n], bank.t[:, 0:n], AF.Ln, [bank.r], [rstd.r], bias=EPS, scale=1.0 / dim)
            act(rstd.t[:, 0:n], rstd.t[:, 0:n], AF.Exp, [rstd.r], [rstd.r], scale=-0.5)

        def conv_fm(buf, hal, hidx, n, wcols, bias_col, out_ap, out_res, func, save_halo=True):
            K = len(wcols)
            S.emit("act", lambda e: e.activation(out=buf.t[:, 0:HALO], in_=hal.t[:, hidx, :], func=AF.Copy),
                   [hal.r], [buf.r])
            acc = cacc[hidx % 2]
            if bias_col is None:
                ts(acc.t[:, 0:n], buf.t[:, HALO:HALO + n], pv(wcols[K - 1]), ALU.mult, [buf.r, pvec.r], [acc.r])
            else:
                ts(acc.t[:, 0:n], buf.t[:, HALO:HALO + n], pv(wcols[K - 1]), ALU.mult, [buf.r, pvec.r], [acc.r],
                   s2=pv(bias_col), op1=ALU.add)
            for d in range(1, K):
                stt(acc.t[:, 0:n], buf.t[:, HALO - d:HALO - d + n], pv(wcols[K - 1 - d]), acc.t[:, 0:n],
                    ALU.mult, ALU.add, [buf.r, pvec.r, acc.r], [acc.r])
            if save_halo:
                S.emit("act", lambda e: e.activation(out=hal.t[:, hidx, :], in_=buf.t[:, n:n + HALO], func=AF.Copy),
                       [buf.r], [hal.r])
            if func is not None:
                act(out_ap, acc.t[:, 0:n], func, [acc.r], [out_res])
            return acc

        def tile(t0, n, mode, premask):
            nch = n // 128
            full = mode == "full"
            for j in range(16):
                S.dma("sp", c_x[j], xb[j].t[:, 0:n], xT[j * 128:(j + 1) * 128, t0:t0 + n], writes=[xb[j].r])
            sumsq_rstd([(xb[j].t[:, 0:n], xb[j].r) for j in range(16)], n, D)
            for j in range(16):
                stt(hb[j].t[:, 0:n], xb[j].t[:, 0:n], pv(PV_PREMIX + j), rstd.t[:, 0:n], ALU.mult, ALU.mult,
                    [xb[j].r, pvec.r, rstd.r], [hb[j].r])

            dslots = [wload(w_in[h * 1024:(h + 1) * 1024, O_DT:O_DT + 32], 32) for h in range(2)]
            for c in range(nch):
                bank = next_pg()
                for kc in range(16):
                    sl = dslots[kc // 8]
                    mm(bank.t[:, 0:32], hb[kc].t[:, c * 128:(c + 1) * 128], sl.t[:, kc % 8, 0:32],
                       start=(kc == 0), stop=(kc == 15), reads=[sl.r, hb[kc].r], writes=[bank.r],
                       signal=(kc % 8 == 7))
                tt(sm1.t[:], bank.t[:, 0:32], hvec.t[:, 0:32], ALU.add, [bank.r, hvec.r], [sm1.r])
                act(sm1.t[:], sm1.t[:], AF.Exp, [sm1.r], [sm1.r])
                act(dtc[c].t[:], sm1.t[:], AF.Ln, [sm1.r], [dtc[c].r], bias=1.0)
                if premask:
                    gc = t0 // 128 + c
                    ts(dtc[c].t[:], dtc[c].t[:], pmask.t[:, gc:gc + 1], ALU.mult, [dtc[c].r, pmask.r], [dtc[c].r])
                tt(ac[c].t[:], dtc[c].t[:], aneg.t[:], ALU.mult, [dtc[c].r, aneg.r], [ac[c].r])
                b2 = next_pg()
                mm(b2.t[:, 0:32], tri_ap, ac[c].t[:], True, True, [cst.r, ac[c].r], [b2.r])
                mm(b2.t[:, 32:64], ones_ap, ac[c].t[:], True, True, [cst.r, ac[c].r], [b2.r])
                cpy(acs[c].t[:], b2.t[:, 0:32], [b2.r], [acs[c].r])
                act(eacs[c].t[:], b2.t[:, 0:32], AF.Exp, [b2.r], [eacs[c].r])
                act(decb[c].t[:], b2.t[:, 32:64], AF.Exp, [b2.r], [decb[c].r])
                tt(sm2.t[:], b2.t[:, 32:64], acs[c].t[:], ALU.subtract, [b2.r, acs[c].r], [sm2.r])
                act(sm2.t[:], sm2.t[:], AF.Exp, [sm2.r], [sm2.r])
                tt(dtdte[c].t[:], sm2.t[:], dtc[c].t[:], ALU.mult, [sm2.r, dtc[c].r], [dtdte[c].r])

            def branchA(jj):
                PC = yield from proj_fm(w_in, 0, 2, O_CG + jj * 256, 256, hb, n)
                for ci in range(2):
                    act(cacc[ci].t[:, 0:n], PC[ci].t[:, 0:n], AF.Copy, [PC[ci].r], [cacc[ci].r])
                    yield
                PVb = yield from proj_fm(w_in, 0, 2, O_V + jj * 256, 256, hb, n)
                for ci in range(2):
                    j = jj * 2 + ci
                    tt(cvb[ci].t[:, HALO:HALO + n], cacc[ci].t[:, 0:n], PVb[ci].t[:, 0:n], ALU.mult,
                       [cacc[ci].r, PVb[ci].r], [cvb[ci].r])
                    yield
                    if full:
                        conv_fm(cvb[ci], halo_cv, j, n, [PV_SCW + k * 16 + j for k in range(3)], None, None, None, None)
                    else:
                        S.emit("act", (lambda ci, j: lambda e: e.activation(
                            out=halo_cv.t[:, j, :], in_=cvb[ci].t[:, n:n + HALO], func=AF.Copy))(ci, j),
                            [cvb[ci].r], [halo_cv.r])
                    yield
                if not full:
                    return
                PB = yield from proj_fm(w_in, 0, 2, O_BG + jj * 256, 256, hb, n)
                for ci in range(2):
                    j = jj * 2 + ci
                    acc = cacc[ci]
                    tt(acc.t[:, 0:n], acc.t[:, 0:n], PB[ci].t[:, 0:n], ALU.mult, [acc.r, PB[ci].r], [acc.r])
                    yield
                    sumsq_rstd([(acc.t[:, 0:n], acc.r)], n, 128)
                    yield
                    stt(mix[j].t[:, 0:n], acc.t[:, 0:n], pv(PV_CNG + j), rstd.t[:, 0:n], ALU.mult, ALU.mult,
                        [acc.r, pvec.r, rstd.r], [mix[j].r])
                    yield

            def prep(g, bs):
                X = xcbS[bs]
                PX = yield from proj_fm(w_in, 0, 2, O_XBC + g * 256, 256, hb, n)
                for i in range(2):
                    chn = 2 * g + i
                    act(X[i].t[:, HALO:HALO + n], PX[i].t[:, 0:n], AF.Copy, [PX[i].r], [X[i].r])
                    yield
                    conv_fm(X[i], halo_x, chn, n, [PV_SSMW + k * 32 + chn for k in range(4)], PV_SSMB + chn,
                            X[i].t[:, HALO:HALO + n], X[i].r, AF.Silu)
                    yield
                PBm = yield from proj_fm(w_in, 0, 2, O_XBC + 2048 + g * 128, 128, hb, n)
                chn = 16 + g
                act(X[2].t[:, HALO:HALO + n], PBm[0].t[:, 0:n], AF.Copy, [PBm[0].r], [X[2].r])
                yield
                conv_fm(X[2], halo_x, chn, n, [PV_SSMW + k * 32 + chn for k in range(4)], PV_SSMB + chn,
                        X[2].t[:, HALO:HALO + n], X[2].r, AF.Silu)
                yield
                if mode != "state":
                    PCm = yield from proj_fm(w_in, 0, 2, O_XBC + 3072 + g * 128, 128, hb, n)
                    chn = 24 + g
                    act(X[3].t[:, HALO:HALO + n], PCm[0].t[:, 0:n], AF.Copy, [PCm[0].r], [X[3].r])
                    yield
                    conv_fm(X[3], halo_x, chn, n, [PV_SSMW + k * 32 + chn for k in range(4)], PV_SSMB + chn,
                            CbS[bs].t[:, 0:n], CbS[bs].r, AF.Silu)
                    yield
                if full:
                    cpy(BbS[bs].t[:, 0:n], X[2].t[:, HALO:HALO + n], [X[2].r], [BbS[bs].r])
                    yield
                    zsl = [wload(w_in[h * 1024:(h + 1) * 1024, O_Z + g * 256:O_Z + (g + 1) * 256], 256)
                           for h in range(2)]
                    for c in range(nch):
                        bank = next_pg()
                        for kc in range(16):
                            sl = zsl[kc // 8]
                            mm(bank.t[:, 0:256], hb[kc].t[:, c * 128:(c + 1) * 128], sl.t[:, kc % 8, 0:256],
                               start=(kc == 0), stop=(kc == 15), reads=[sl.r, hb[kc].r], writes=[bank.r],
                               signal=(kc % 8 == 7))
                            if kc % KWG == KWG - 1:
                                yield (kc == 15)
                        act(zsS[bs][c].t[:], bank.t[:, 0:256], AF.Silu, [bank.r], [zsS[bs][c].r])
                        yield

            def ssd(g, bs):
                X = xcbS[bs]
                Bb, Cb, zs = BbS[bs], CbS[bs], zsS[bs]
                gs = slice(g * 256, (g + 1) * 256)
                h4 = slice(g * 4, (g + 1) * 4)

                def bc4(tl):
                    return tl.t[:, h4].unsqueeze(2).to_broadcast([128, 4, 64])

                def v3(ap):
                    return ap.rearrange("p (h j) -> p h j", h=4)

                if not full:
                    def tps(c):
                        pA = pS0 if c % 2 == 0 else pS1
                        xc = slice(HALO + c * 128, HALO + (c + 1) * 128)
                        for i in range(2):
                            tp(pA.t[:, i * 128:(i + 1) * 128], X[i].t[:, xc], [X[i].r], [pA.r])
                        tp(pA.t[:, 256:384], X[2].t[:, xc], [X[2].r], [pA.r])
                    yield True
                    tps(0)
                    yield
                    for c in range(nch):
                        pA, pB = (pS0, pS3) if c % 2 == 0 else (pS1, pS2)
                        Btk, xdc = B_tokS[c % 2], xdecS[c % 2]
                        if c + 1 < nch:
                            yield True
                            tps(c + 1)
                            yield
                        act(Btk.t[:], pA.t[:, 256:384], AF.Copy, [pA.r], [Btk.r])
                        yield
                        tt(v3(xdc.t[:]), v3(pA.t[:, 0:256]), bc4(dtdte[c]), ALU.mult, [pA.r, dtdte[c].r], [xdc.r])
                        yield
                        mm(pB.t[:, 0:256], Btk.t[:], xdc.t[:], True, True, [Btk.r, xdc.r], [pB.r])
                        yield
                        tt(v3(hstate.t[:, gs]), v3(hstate.t[:, gs]), bc4(decb[c]), ALU.mult, [r_hs[g], decb[c].r], [r_hs[g]])
                        yield
                        tt(hstate.t[:, gs], hstate.t[:, gs], pB.t[:, 0:256], ALU.add, [r_hs[g], pB.r], [r_hs[g]])
                        yield
                        if mode != "state":
                            act(hprev.t[:, gs], hstate.t[:, gs], AF.Copy, [r_hs[g]], [r_hp[g]])
                            yield
                    return
                for c in range(nch):
                    cs = slice(c * 128, (c + 1) * 128)
                    xc = slice(HALO + c * 128, HALO + (c + 1) * 128)
                    if full:
                        pA, pB, Btk, xdc = pS0, pS3, B_tok, xdec
                    else:
                        pA, pB = (pS0, pS3) if c % 2 == 0 else (pS1, pS2)
                        Btk, xdc = B_tokS[c % 2], xdecS[c % 2]
                    yield True
                    for i in range(2):
                        tp(pA.t[:, i * 128:(i + 1) * 128], X[i].t[:, xc], [X[i].r], [pA.r])
                    tp(pA.t[:, 256:384], X[2].t[:, xc], [X[2].r], [pA.r])
                    yield
                    act(Btk.t[:], pA.t[:, 256:384], AF.Copy, [pA.r], [Btk.r])
                    yield
                    if full:
                        act(x_tok.t[:], pA.t[:, 0:256], AF.Copy, [pA.r], [x_tok.r])
                        yield
                        tt(v3(xdc.t[:]), v3(x_tok.t[:]), bc4(dtdte[c]), ALU.mult, [x_tok.r, dtdte[c].r], [xdc.r])
                    else:
                        tt(v3(xdc.t[:]), v3(pA.t[:, 0:256]), bc4(dtdte[c]), ALU.mult, [pA.r, dtdte[c].r], [xdc.r])
                    yield
                    mm(pB.t[:, 0:256], Btk.t[:], xdc.t[:], True, True, [Btk.r, xdc.r], [pB.r])
                    yield
                    if full:
                        tt(v3(xdt.t[:]), v3(x_tok.t[:]), bc4(dtc[c]), ALU.mult, [x_tok.r, dtc[c].r], [xdt.r])
                        yield
                        mm(pS0.t[:, 384:512], Bb.t[:, cs], Cb.t[:, cs], True, True, [Bb.r, Cb.r], [r_CB])
                        yield
                        tt(CBm.t[:], pS0.t[:, 384:512], tri_ap, ALU.mult, [r_CB, cst.r], [CBm.r])
                        yield
                        mm(pS2.t[:, 256:512], Cb.t[:, cs], hprev.t[:, gs], True, True, [Cb.r, r_hp[g]], [r_yo])
                        yield
                        tt(lhD4.t[:].rearrange("p (h s) -> p h s", h=4),
                           strict_ap.unsqueeze(1).to_broadcast([128, 4, 128]),
                           ac[c].t[:, h4].unsqueeze(2).to_broadcast([128, 4, 128]), ALU.mult,
                           [cst.r, ac[c].r], [lhD4.r])
                        yield True
                        for r in range(4):
                            mm(pS1.t[:, r * 128:(r + 1) * 128], lhD4.t[:, r * 128:(r + 1) * 128], tri_ap, True, True,
                               [lhD4.r, cst.r], [pS1.r])
                        yield
                        act(Eh4.t[:], pS1.t[:], AF.Exp, [pS1.r], [Eh4.r])
                        yield
                        tt(scT4.t[:].rearrange("p (h s) -> p h s", h=4), Eh4.t[:].rearrange("p (h s) -> p h s", h=4),
                           CBm.t[:].unsqueeze(1).to_broadcast([128, 4, 128]), ALU.mult, [Eh4.r, CBm.r], [scT4.r])
                        yield
                        for r in range(4):
                            mm(pS2.t[:, r * 64:(r + 1) * 64], scT4.t[:, r * 128:(r + 1) * 128],
                               xdt.t[:, r * 64:(r + 1) * 64], True, True, [scT4.r, xdt.r], [r_yd])
                        yield
                        tt(v3(ysb.t[:]), v3(pS2.t[:, 256:512]), bc4(eacs[c]), ALU.mult, [r_yo, eacs[c].r], [ysb.r])
                        yield
                        tt(ysb.t[:], ysb.t[:], pS2.t[:, 0:256], ALU.add, [ysb.r, r_yd], [ysb.r])
                        yield
                        tt(v3(yt2.t[:]), v3(x_tok.t[:]),
                           hvec.t[:, 64 + g * 4:64 + (g + 1) * 4].unsqueeze(2).to_broadcast([128, 4, 64]),
                           ALU.mult, [x_tok.r, hvec.r], [yt2.r])
                        yield
                        tt(ysb.t[:], ysb.t[:], yt2.t[:], ALU.add, [ysb.r, yt2.r], [ysb.r])
                        yield
                        tt(ysb.t[:], ysb.t[:], zs[c].t[:], ALU.mult, [ysb.r, zs[c].r], [ysb.r])
                        yield
                        act(yt2.t[:], ysb.t[:], AF.Square, [ysb.r], [yt2.r, ssq.r], accum=ssq.t[:])
                        yield
                        act(grs.t[:], ssq.t[:], AF.Ln, [ssq.r], [grs.r], bias=EPS, scale=1.0 / 256)
                        act(grs.t[:], grs.t[:], AF.Exp, [grs.r], [grs.r], scale=-0.5)
                        yield
                        ts(yt2.t[:], ysb.t[:], grs.t[:, 0:1], ALU.mult, [ysb.r, grs.r], [yt2.r])
                        yield
                        yield True
                        for i in range(2):
                            tp(pS3.t[:, 256 + i * 128:256 + (i + 1) * 128], yt2.t[:, i * 128:(i + 1) * 128],
                               [yt2.r], [r_yT])
                        yield
                        for i in range(2):
                            act(mix[16 + 2 * g + i].t[:, cs], pS3.t[:, 256 + i * 128:256 + (i + 1) * 128], AF.Copy,
                                [r_yT, pvec.r], [mix[16 + 2 * g + i].r], scale=pv(PV_SNG + 2 * g + i))
                        yield
                    tt(v3(hstate.t[:, gs]), v3(hstate.t[:, gs]), bc4(decb[c]), ALU.mult, [r_hs[g], decb[c].r], [r_hs[g]])
                    yield
                    tt(hstate.t[:, gs], hstate.t[:, gs], pB.t[:, 0:256], ALU.add, [r_hs[g], pB.r], [r_hs[g]])
                    yield
                    if mode != "state":
                        act(hprev.t[:, gs], hstate.t[:, gs], AF.Copy, [r_hs[g]], [r_hp[g]])
                        yield

            run(prep(0, 0))
            for g in range(NG):
                mains = []
                if g + 1 < NG:
                    mains.append(prep(g + 1, (g + 1) % 2))
                if mode != "state":
                    mains.append(branchA(g))
                weave(chain(*mains), ssd(g, g % 2))
            if not full or KSTOP <= 4:
                return

            for cb in range(8):
                banks = [next_pg() for _ in range(2)]
                for hf in range(4):
                    run(proj_fm(w_out, hf * 1024, 1, cb * 256, 256, mix[hf * 8:(hf + 1) * 8], n, banks=banks,
                                first=(hf == 0), last=(hf == 3)))
                for ci in range(2):
                    act(ob[cb * 2 + ci].t[:, 0:n], banks[ci].t[:, 0:n], AF.Copy, [banks[ci].r], [ob[cb * 2 + ci].r])
            sumsq_rstd([(ob[j].t[:, 0:n], ob[j].r) for j in range(16)], n, D)
            for j in range(16):
                stt(ob[j].t[:, 0:n], ob[j].t[:, 0:n], pv(PV_POSTMIX + j), rstd.t[:, 0:n], ALU.mult, ALU.mult,
                    [ob[j].r, pvec.r, rstd.r], [ob[j].r])
                tt(xb[j].t[:, 0:n], xb[j].t[:, 0:n], ob[j].t[:, 0:n], ALU.add, [xb[j].r, ob[j].r], [xb[j].r])
            sumsq_rstd([(xb[j].t[:, 0:n], xb[j].r) for j in range(16)], n, D)
            for j in range(16):
                stt(hb[j].t[:, 0:n], xb[j].t[:, 0:n], pv(PV_PREMLP + j), rstd.t[:, 0:n], ALU.mult, ALU.mult,
                    [xb[j].r, pvec.r, rstd.r], [hb[j].r])
            for half in range(2):
                for cb in range(16):
                    banks = run(proj_fm(w_ff1, 0, 2, half * 4096 + cb * 256, 256, hb, n))
                    for ci in range(2):
                        m = mix[cb * 2 + ci]
                        act(cvb[ci].t[:, 0:n], banks[ci].t[:, 0:n], AF.Relu, [banks[ci].r], [cvb[ci].r])
                        tt(m.t[:, 0:n], cvb[ci].t[:, 0:n], cvb[ci].t[:, 0:n], ALU.mult, [cvb[ci].r], [m.r])
                for cb in range(8):
                    banks = [next_pg() for _ in range(2)]
                    for hf in range(4):
                        run(proj_fm(w_ff2, half * 4096 + hf * 1024, 1, cb * 256, 256, mix[hf * 8:(hf + 1) * 8], n,
                                    banks=banks, first=(hf == 0), last=(hf == 3)))
                    for ci in range(2):
                        o = ob[cb * 2 + ci]
                        if half == 0:
                            act(o.t[:, 0:n], banks[ci].t[:, 0:n], AF.Copy, [banks[ci].r], [o.r])
                        else:
                            tt(o.t[:, 0:n], o.t[:, 0:n], banks[ci].t[:, 0:n], ALU.add, [o.r, banks[ci].r], [o.r])
            sumsq_rstd([(ob[j].t[:, 0:n], ob[j].r) for j in range(16)], n, D)
            for j in range(16):
                stt(ob[j].t[:, 0:n], ob[j].t[:, 0:n], pv(PV_POSTMLP + j), rstd.t[:, 0:n], ALU.mult, ALU.mult,
                    [ob[j].r, pvec.r, rstd.r], [ob[j].r])
                tt(ob[j].t[:, 0:n], ob[j].t[:, 0:n], xb[j].t[:, 0:n], ALU.add, [ob[j].r, xb[j].r], [ob[j].r])
                S.dma("sp", c_o[j], outT[j * 128:(j + 1) * 128, t0 - PRE - PFX:t0 - PRE - PFX + n], ob[j].t[:, 0:n],
                      reads=[ob[j].r])

        if exchange and ntile >= 0:
            tile(0, PRE, "state", True)
            for ti in range(ntile):
                tile(PRE + ti * NT, NT, "state", False)
            c_ex = S.chan()
            S.dma("sp", c_ex, ex_src[:, 0:DSSM], hstate.t[:], reads=r_hs)
            S.dma("sp", c_ex, ex_src[:, DSSM:XW], ltot.t[:], reads=[ltot.r])
            r_exd = Res()
            c_cc = S.chan()
            S.wait("pool", [(c_ex.key, c_ex.cnt)])
            S.raw("pool", c_cc, lambda e: e.collective_compute(
                "AllGather", ALU.bypass, replica_groups=[list(range(NCORE))],
                ins=[ex_src], outs=[ex_dst]), writes=[r_exd])
            lall = sb("lall", [128, NCORE, NH], F32)
            for j in range(NCORE):
                S.dma("sp", c_ex, lall.t[:, j, :], ex_dst[j * 128:(j + 1) * 128, DSSM:XW], reads=[r_exd],
                      writes=[lall.r])
            coef = sb("coef", [128, NH], F32)
            sj = ob
            S.emit("dve", lambda e: e.memset(hstate.t[:], 0.0), [], r_hs)
            for j in range(NCORE):
                ts(coef.t[:], lall.t[:, 0, :], xmask.t[:, 8 + j * 8:8 + j * 8 + 1], ALU.mult,
                   [lall.r, xmask.r], [coef.r])
                for k in range(1, NCORE):
                    stt(coef.t[:], lall.t[:, k, :], xmask.t[:, 8 + j * 8 + k:8 + j * 8 + k + 1], coef.t[:],
                        ALU.mult, ALU.add, [lall.r, xmask.r, coef.r], [coef.r])
                act(coef.t[:], coef.t[:], AF.Exp, [coef.r], [coef.r])
                ts(coef.t[:], coef.t[:], xmask.t[:, j:j + 1], ALU.mult, [coef.r, xmask.r], [coef.r])
                for qd in range(4):
                    S.dma("sp", c_ex, sj[qd].t[:, 0:512], ex_dst[j * 128:(j + 1) * 128, qd * 512:(qd + 1) * 512],
                          reads=[r_exd], writes=[sj[qd].r])
                    tt(sj[qd].t[:, 0:512].rearrange("p (h j) -> p h j", h=8),
                       sj[qd].t[:, 0:512].rearrange("p (h j) -> p h j", h=8),
                       coef.t[:, qd * 8:(qd + 1) * 8].unsqueeze(2).to_broadcast([128, 8, 64]), ALU.mult,
                       [sj[qd].r, coef.r], [sj[qd].r])
                    tt(hstate.t[:, qd * 512:(qd + 1) * 512], hstate.t[:, qd * 512:(qd + 1) * 512], sj[qd].t[:, 0:512],
                       ALU.add, [r_hs[2 * qd], r_hs[2 * qd + 1], sj[qd].r], [r_hs[2 * qd], r_hs[2 * qd + 1]])
            for g in range(NG):
                act(hprev.t[:, g * 256:(g + 1) * 256], hstate.t[:, g * 256:(g + 1) * 256], AF.Copy, [r_hs[g]], [r_hp[g]])
            S.emit("dve", lambda e: e.memset(halo_cv.t[:], 0.0), [], [halo_cv.r])
            S.emit("dve", lambda e: e.memset(halo_x.t[:], 0.0), [], [halo_x.r])

        if not exchange and ntile >= 0:
            npf = NPFXT if ntile == NTILE else int(os.environ.get("KPFX", "0"))
            for ti in range(NPFXT - npf, NPFXT):
                tile(ti * NT, NT, "state", True)
        if ntile >= 0:
            tile(PFX, PRE, "pre", True)
        for ti in range(ntile):
            tile(PFX + PRE + ti * NT, NT, "full", False)
        S.wait("sp", [(c.key, c.cnt) for c in c_o])
        print("instructions:", S.nins, {e: len(S.q[e]) for e in ENGS}, flush=True)
        S.run(st)
    return nc


_CACHE = {}


def _host_consts():
    k = np.arange(128)
    tri = (k[:, None] <= k[None, :]).astype(np.float32)
    strict = (k[:, None] > k[None, :]).astype(np.float32)
    ones = np.ones((128, 128), np.float32)
    ident = np.eye(128, dtype=np.float32)
    return np.ascontiguousarray(np.concatenate([tri, strict, ones, ident], axis=1))


def _fm(v):
    return np.ascontiguousarray(np.asarray(v, np.float32).reshape(-1, 128).T)


def kernel(x, meta_tokens, w_in, short_conv_w, conv_norm_g, ssm_conv_w, ssm_conv_b,
           dt_bias, a_log, d_skip, ssm_norm_g, w_out, pre_mix_g, post_mix_g,
           pre_mlp_g, post_mlp_g, w_ff1, w_ff2, _exchange=False, _ntile=NTILE):
    x = np.asarray(x, np.float32)
    meta = np.asarray(meta_tokens, np.float32)
    key = (_exchange, _ntile)
    if key not in _CACHE:
        _CACHE[key] = build_program(_exchange, _ntile)
    nc = _CACHE[key]
    pvec = np.concatenate(
        [_fm(pre_mix_g[0]), _fm(post_mix_g[0]), _fm(pre_mlp_g[0]), _fm(post_mlp_g[0]), _fm(conv_norm_g[0])]
        + [_fm(short_conv_w[0, k]) for k in range(3)]
        + [_fm(ssm_conv_w[0, k]) for k in range(4)]
        + [_fm(ssm_conv_b[0]), _fm(ssm_norm_g[0])], axis=1)
    assert pvec.shape == (128, PV_N), pvec.shape
    pvec = np.ascontiguousarray(pvec, dtype=np.float32)
    hvec = np.concatenate([np.asarray(dt_bias[0]), np.asarray(a_log[0]), np.asarray(d_skip[0])]).astype(np.float32)
    cst = _host_consts()
    W_in = np.ascontiguousarray(np.asarray(w_in[0], np.float32))
    W_out = np.ascontiguousarray(np.asarray(w_out[0], np.float32))
    W1 = np.ascontiguousarray(np.asarray(w_ff1[0], np.float32))
    W2 = np.ascontiguousarray(np.asarray(w_ff2[0], np.float32))
    in_maps = []
    for c in range(NCORE):
        b, q = divmod(c, NCORE // NB)
        L = PFX + PRE + TPC
        nreal = (q + 1) * TPC
        xt = np.zeros((D, L), np.float32)
        xt[:, L - nreal:] = x[b, :nreal].T
        xt[:, L - nreal - 16:L - nreal] = meta.T
        tm = np.zeros(PFX + PRE, np.float32)
        tm[L - nreal - 16:] = 1.0
        pm = np.ascontiguousarray(tm.reshape(NMASKC, 128).T)
        sel = np.zeros(8, np.float32)
        M = np.zeros((8, 8), np.float32)
        for j in range(NCORE):
            if j // 4 == b and j < c:
                sel[j] = 1.0
                for k2 in range(j + 1, c):
                    M[j, k2] = 1.0
        xm = np.concatenate([sel, M.reshape(-1)]).astype(np.float32)
        in_maps.append({"xT": xt, "w_in": W_in, "w_out": W_out, "w_ff1": W1, "w_ff2": W2, "pvec": pvec,
                        "hvec": hvec, "cst": cst, "pmask": pm, "xmask": xm})
    res = run_bass_kernel_spmd(nc, in_maps, core_ids=list(range(NCORE)))
    out = np.empty((NB, SEQ, D), np.float32)
    nt = _ntile * NT
    for c in range(NCORE):
        b, q = divmod(c, NCORE // NB)
        out[b, q * TPC:q * TPC + nt] = res.results[c]["outT"][:, :nt].T
    return out
```
